# Optimizing a Trainium2 kernel written in Bass

```python
import math
import jax, jax.numpy as jnp
from jax import lax
import numpy as np

D_MODEL = 2048
BATCH = 1
SEQ = 8192
DEPTH = 1

BRANCH_WIDTH = 1024
N_BRANCH = 3
A_HEADS = 8
A_KV_HEADS = 2
A_HEAD_DIM = 128
A_WIDTH = A_HEADS * A_HEAD_DIM
IDX_HEADS = 16
IDX_DIM = 64
TOPK_MAX = 256
IDX_SCALE = (IDX_DIM ** -0.5) * (IDX_HEADS ** -0.5)
A_SCALE = A_HEAD_DIM ** -0.5
B_HEADS = 8
B_Q_RANK = 512
B_KV_RANK = 256
B_NOPE = 128
B_ROPE = 64
B_QK_DIM = B_NOPE + B_ROPE
B_V = 128
B_WIDTH = B_HEADS * B_V
B_SCALE = B_QK_DIM ** -0.5
ROPE_THETA = 10000.0
N_MEM = 256
C_HEADS = 4
C_HEAD_DIM = 256
C_WIDTH = C_HEADS * C_HEAD_DIM
C_SCALE = C_HEAD_DIM ** -0.5
REL_BUCKETS = 32
REL_MAX_DIST = 128
Q_BLOCK = 128
EPS = 1e-6

IN_SIZES = (
    A_WIDTH,
    A_KV_HEADS * A_HEAD_DIM,
    A_KV_HEADS * A_HEAD_DIM,
    A_WIDTH,
    IDX_HEADS * IDX_DIM,
    IDX_DIM,
    IDX_HEADS,
    B_Q_RANK,
    B_KV_RANK,
    B_ROPE,
    B_WIDTH,
    C_WIDTH,
    C_WIDTH,
    N_BRANCH * D_MODEL,
)
IN_WIDTH = int(sum(IN_SIZES))
IN_OFFSETS = [int(o) for o in np.cumsum(IN_SIZES)[:-1]]

kernel_name = "hybrid_dsa_mla_memory_gated_block"


def rms_norm(x, g):
    x32 = x.astype(jnp.float32)
    y = x32 * lax.rsqrt(jnp.mean(x32 * x32, axis=-1, keepdims=True) + EPS)
    return (y * g.astype(jnp.float32)).astype(x.dtype)


def apply_rope(x, cos, sin):
    x32 = x.astype(jnp.float32)
    half = x.shape[-1] // 2
    x1, x2 = x32[..., :half], x32[..., half:]
    return jnp.concatenate([x1 * cos - x2 * sin, x2 * cos + x1 * sin], axis=-1).astype(x.dtype)


def t5_bucket(dist):
    n = jnp.maximum(dist, 0)
    max_exact = REL_BUCKETS // 2
    nf = jnp.maximum(n, 1).astype(jnp.float32)
    large = max_exact + (jnp.log(nf / max_exact) / math.log(REL_MAX_DIST / max_exact)
                         * (REL_BUCKETS - max_exact)).astype(jnp.int32)
    large = jnp.minimum(large, REL_BUCKETS - 1)
    return jnp.where(n < max_exact, n, large)


_gather_rows = jax.vmap(lambda a, i: a[i])


def dsa_attention(q, k, v, q_idx, k_idx, w_idx, pos, rel_table):
    bsz, seq = q.shape[0], q.shape[1]
    topk = min(TOPK_MAX, seq // 4)
    n_blocks = seq // Q_BLOCK
    rep = A_HEADS // A_KV_HEADS
    key_ids = jnp.arange(seq)
    k_idx32 = k_idx.astype(jnp.float32)

    def block(i):
        start = i * Q_BLOCK
        qi = lax.dynamic_slice_in_dim(q_idx, start, Q_BLOCK, axis=1).astype(jnp.float32)
        wi = lax.dynamic_slice_in_dim(w_idx, start, Q_BLOCK, axis=1).astype(jnp.float32)
        qb = lax.dynamic_slice_in_dim(q, start, Q_BLOCK, axis=1)
        pq = lax.dynamic_slice_in_dim(pos, start, Q_BLOCK, axis=1)
        t_ids = start + jnp.arange(Q_BLOCK)
        causal = key_ids[None, :] <= t_ids[:, None]
        dots = jnp.einsum('bthd,bsd->bths', qi, k_idx32)
        score = jnp.einsum('bths,bth->bts', jax.nn.relu(dots), wi) * IDX_SCALE
        score = jnp.where(causal[None], score, -jnp.inf)
        _, sel = lax.top_k(score, topk)
        kg = _gather_rows(k, sel)
        vg = _gather_rows(v, sel)
        pk = _gather_rows(pos, sel)
        qg = qb.reshape(bsz, Q_BLOCK, A_KV_HEADS, rep, A_HEAD_DIM)
        logits = jnp.einsum('btgrd,btkgd->btgrk', qg, kg).astype(jnp.float32) * A_SCALE
        bias = rel_table[t5_bucket(pq[:, :, None] - pk)].astype(jnp.float32)
        bias = bias.reshape(bsz, Q_BLOCK, topk, A_KV_HEADS, rep).transpose(0, 1, 3, 4, 2)
        valid = sel <= t_ids[None, :, None]
        logits = jnp.where(valid[:, :, None, None, :], logits + bias, -jnp.inf)
        p = jax.nn.softmax(logits, axis=-1).astype(v.dtype)
        o = jnp.einsum('btgrk,btkgd->btgrd', p, vg)
        return o.reshape(bsz, Q_BLOCK, A_HEADS, A_HEAD_DIM)

    out = lax.map(block, jnp.arange(n_blocks))
    return jnp.moveaxis(out, 0, 1).reshape(bsz, seq, A_HEADS, A_HEAD_DIM)


def mla_attention(q, k, v):
    bsz, seq = q.shape[0], q.shape[1]
    n_blocks = seq // Q_BLOCK
    key_ids = jnp.arange(seq)

    def block(i):
        start = i * Q_BLOCK
        qb = lax.dynamic_slice_in_dim(q, start, Q_BLOCK, axis=1)
        logits = jnp.einsum('bthd,bshd->bhts', qb, k).astype(jnp.float32) * B_SCALE
        causal = key_ids[None, :] <= (start + jnp.arange(Q_BLOCK))[:, None]
        logits = jnp.where(causal, logits, -jnp.inf)
        p = jax.nn.softmax(logits, axis=-1).astype(v.dtype)
        return jnp.einsum('bhts,bshd->bthd', p, v)

    out = lax.map(block, jnp.arange(n_blocks))
    return jnp.moveaxis(out, 0, 1).reshape(bsz, seq, q.shape[2], v.shape[-1])


def memory_attention(q, k, v):
    logits = jnp.einsum('bshd,bmhd->bhsm', q, k).astype(jnp.float32) * C_SCALE
    p = jax.nn.softmax(logits, axis=-1).astype(v.dtype)
    return jnp.einsum('bhsm,bmhd->bshd', p, v)


def setup_inputs(seed: int = 0) -> dict:
    key = jax.random.key(seed)
    ks = jax.random.split(key, 24)
    f32 = jnp.float32

    def nrm(k, shape, fan_in):
        return jax.random.normal(k, shape, f32) * (fan_in ** -0.5)

    def gain(k, shape):
        return 1.0 + 0.02 * jax.random.normal(k, shape, f32)

    x = jax.random.normal(ks[0], (BATCH, SEQ, D_MODEL), f32)
    mem = jax.random.normal(ks[1], (BATCH, N_MEM, D_MODEL), f32)
    positions = jnp.broadcast_to(jnp.arange(SEQ, dtype=jnp.int32)[None, :], (BATCH, SEQ))
    rel_bias = 0.5 * jax.random.normal(ks[2], (REL_BUCKETS, A_HEADS), f32)
    return {
        "x": x,
        "mem": mem,
        "positions": positions,
        "rel_bias": rel_bias,
        "norm_g": gain(ks[3], (DEPTH, D_MODEL)),
        "mem_norm_g": gain(ks[4], (DEPTH, D_MODEL)),
        "w_in": nrm(ks[5], (DEPTH, D_MODEL, IN_WIDTH), D_MODEL),
        "a_q_norm_g": gain(ks[6], (DEPTH, A_HEAD_DIM)),
        "a_k_norm_g": gain(ks[7], (DEPTH, A_HEAD_DIM)),
        "b_q_lat_norm_g": gain(ks[8], (DEPTH, B_Q_RANK)),
        "b_kv_lat_norm_g": gain(ks[9], (DEPTH, B_KV_RANK)),
        "w_b_uq": nrm(ks[10], (DEPTH, B_Q_RANK, B_HEADS * B_QK_DIM), B_Q_RANK),
        "w_b_ukv": nrm(ks[11], (DEPTH, B_KV_RANK, B_HEADS * (B_NOPE + B_V)), B_KV_RANK),
        "b_q_norm_g": gain(ks[12], (DEPTH, B_QK_DIM)),
        "b_k_norm_g": gain(ks[13], (DEPTH, B_QK_DIM)),
        "w_mem_kv": nrm(ks[14], (DEPTH, D_MODEL, 2 * C_WIDTH), D_MODEL),
        "c_q_norm_g": gain(ks[15], (DEPTH, C_HEAD_DIM)),
        "c_k_norm_g": gain(ks[16], (DEPTH, C_HEAD_DIM)),
        "w_branch": nrm(ks[17], (DEPTH, N_BRANCH, BRANCH_WIDTH, D_MODEL), BRANCH_WIDTH),
        "w_out": nrm(ks[18], (DEPTH, D_MODEL, D_MODEL), D_MODEL),
    }


def reference(x, mem, positions, rel_bias, norm_g, mem_norm_g, w_in, a_q_norm_g, a_k_norm_g,
              b_q_lat_norm_g, b_kv_lat_norm_g, w_b_uq, w_b_ukv, b_q_norm_g, b_k_norm_g,
              w_mem_kv, c_q_norm_g, c_k_norm_g, w_branch, w_out):
    bsz, seq, d = x.shape
    n_mem = mem.shape[1]
    inv_freq = 1.0 / (ROPE_THETA ** (jnp.arange(0, B_ROPE, 2, dtype=jnp.float32) / B_ROPE))
    ang = positions.astype(jnp.float32)[..., None] * inv_freq
    cos = jnp.cos(ang)[:, :, None, :]
    sin = jnp.sin(ang)[:, :, None, :]

    for l in range(DEPTH):
        h = rms_norm(x, norm_g[l])
        proj = h @ w_in[l]
        (aq, ak, av, az, iq, ik, iw, bcq, bckv, bkpe, bz, cq, cz, gates) = jnp.split(
            proj, IN_OFFSETS, axis=-1)

        aq = rms_norm(aq.reshape(bsz, seq, A_HEADS, A_HEAD_DIM), a_q_norm_g[l])
        ak = rms_norm(ak.reshape(bsz, seq, A_KV_HEADS, A_HEAD_DIM), a_k_norm_g[l])
        av = av.reshape(bsz, seq, A_KV_HEADS, A_HEAD_DIM)
        iq = iq.reshape(bsz, seq, IDX_HEADS, IDX_DIM)
        o_a = dsa_attention(aq, ak, av, iq, ik, iw, positions, rel_bias).reshape(bsz, seq, A_WIDTH)

        qb = (rms_norm(bcq, b_q_lat_norm_g[l]) @ w_b_uq[l]).reshape(bsz, seq, B_HEADS, B_QK_DIM)
        qb = rms_norm(qb, b_q_norm_g[l])
        qb = jnp.concatenate([qb[..., :B_NOPE], apply_rope(qb[..., B_NOPE:], cos, sin)], axis=-1)
        kv = (rms_norm(bckv, b_kv_lat_norm_g[l]) @ w_b_ukv[l]).reshape(bsz, seq, B_HEADS, B_NOPE + B_V)
        k_nope, vb = kv[..., :B_NOPE], kv[..., B_NOPE:]
        k_pe = jnp.broadcast_to(bkpe[:, :, None, :], (bsz, seq, B_HEADS, B_ROPE))
        kb = rms_norm(jnp.concatenate([k_nope, k_pe], axis=-1), b_k_norm_g[l])
        kb = jnp.concatenate([kb[..., :B_NOPE], apply_rope(kb[..., B_NOPE:], cos, sin)], axis=-1)
        o_b = mla_attention(qb, kb, vb).reshape(bsz, seq, B_WIDTH)

        mkv = rms_norm(mem, mem_norm_g[l]) @ w_mem_kv[l]
        ck = rms_norm(mkv[..., :C_WIDTH].reshape(bsz, n_mem, C_HEADS, C_HEAD_DIM), c_k_norm_g[l])
        cv = mkv[..., C_WIDTH:].reshape(bsz, n_mem, C_HEADS, C_HEAD_DIM)
        cq = rms_norm(cq.reshape(bsz, seq, C_HEADS, C_HEAD_DIM), c_q_norm_g[l])
        o_c = memory_attention(cq, ck, cv).reshape(bsz, seq, C_WIDTH)

        u = jnp.stack([o_a * jax.nn.silu(az), o_b * jax.nn.silu(bz), o_c * jax.nn.silu(cz)],
                      axis=2)
        y_br = jnp.einsum('bsnw,nwd->bsnd', u, w_branch[l])
        g = jax.nn.sigmoid(gates.reshape(bsz, seq, N_BRANCH, d))
        merged = jnp.einsum('bsnd,bsnd->bsd', g, y_br)
        x = x + merged @ w_out[l]
    return x
```

```python
import math
from contextlib import ExitStack

import numpy as np
import concourse.bass as bass
import concourse.mybir as mybir
from concourse.bass_utils import run_bass_kernel_spmd

F32 = mybir.dt.float32
BF16 = mybir.dt.bfloat16
I32 = mybir.dt.int32
ALU = mybir.AluOpType
AF = mybir.ActivationFunctionType
AX = mybir.AxisListType

NCORES = 8
SEQ = 8192
D = 2048
KC = 16
OWN = SEQ // NCORES
NSLOT = OWN // 128
NB = SEQ // 128
EPS = 1e-6
NITER = 16
TOPK = 256
NEG = -30000.0

O_AQ, O_AK, O_AV, O_AZ, O_IQ, O_IK, O_IW = 0, 1024, 1280, 1536, 2560, 3584, 3648
O_BCQ, O_BCKV, O_BKPE, O_BZ, O_CQ, O_CZ, O_G = 3664, 4176, 4432, 4496, 5520, 6544, 7568
IN_W = 13712

A_SCALE = 128 ** -0.5
B_SCALE = 192 ** -0.5
C_SCALE = 256 ** -0.5
TWO_PI = 2.0 * math.pi
CW1 = 6.28125
CW2 = TWO_PI - CW1

C_ID = 0
C_CM = C_ID + 128
C_BK = C_CM + 1024
C_G = C_BK + 256
G_NORM = C_G
G_MEM = G_NORM + 16
G_AQ = G_MEM + 16
G_AK = G_AQ + 1
G_QLAT = G_AK + 1
G_KVLAT = G_QLAT + 4
G_BQN = G_KVLAT + 2
G_BQR = G_BQN + 1
G_BKN = G_BQR + 1
G_BKR = G_BKN + 1
G_CQ = G_BKR + 1
G_CK = G_CQ + 2
C_INVF = G_CK + 2
C_PM = C_INVF + 1
C_SEL = C_PM + 64
C_PW = C_SEL + 18
NCST = C_PW + NITER


class Buf:
    __slots__ = ("name", "last_w", "readers", "dsem", "dcount")

    def __init__(self, name):
        self.name = name
        self.last_w = None
        self.readers = []
        self.dsem = None
        self.dcount = 0


class Sched:
    def __init__(self, nc, stack):
        self.nc = nc
        self.stack = stack
        self.sems = {}
        self.streams = {k: [] for k in ("pe", "act", "dve", "pool", "sp")}
        self.count = {k: 0 for k in self.streams}
        self.known = {k: {} for k in self.streams}
        for k in self.streams:
            self._sem("eng_" + k)
        self.store_count = 0
        self._sem("store")
        self.dbufs = []
        self.n_ins = 0
        self.scope = None
        self.use_scopes = False

    def _sem(self, key):
        if key not in self.sems:
            self.sems[key] = self.stack.enter_context(self.nc.semaphore(key))
        return key

    def buf(self, name, dma=False):
        b = Buf(name)
        if dma:
            b.dsem = self._sem("d_" + name)
            self.dbufs.append(b)
        return b

    def _deps(self, eng, reads, writes):
        deps = {}
        own = "eng_" + eng

        def add(ev, raw):
            if ev is None:
                return
            k, v = ev
            if k == "store":
                v = self.store_count
            if k == own:
                if eng in ("pe", "sp"):
                    return
                if not raw:
                    return
            if deps.get(k, 0) < v:
                deps[k] = v
        for b in reads:
            add(b.last_w, True)
        for b in writes:
            add(b.last_w, False)
            for r in b.readers:
                add(r, False)
        out = []
        kn = self.known[eng]
        for k, v in deps.items():
            if kn.get(k, 0) >= v:
                continue
            kn[k] = v
            out.append((k, v))
        return out

    def op(self, eng, fn, reads=(), writes=()):
        waits = self._deps(eng, reads, writes)
        self.count[eng] += 1
        ev = ("eng_" + eng, self.count[eng])
        self.streams[eng].append((waits, fn, [("eng_" + eng, 1)], self.scope))
        for b in reads:
            b.readers.append(ev)
        for b in writes:
            b.last_w = ev
            b.readers = []
        self.n_ins += 1
        return ev

    def dma(self, fn, reads=(), writes=(), q="sp"):
        waits = self._deps(q, reads, writes)
        dst = [b for b in writes if b.dsem is not None]
        if dst:
            b0 = dst[0]
            b0.dcount += 16
            ev = (b0.dsem, b0.dcount)
            incs = [(b0.dsem, 16)]
        else:
            self.store_count += 16
            ev = ("store", self.store_count)
            incs = [("store", 16)]
        self.streams[q].append((waits, fn, incs, self.scope))
        for b in reads:
            b.readers.append(ev)
        for b in writes:
            b.last_w = ev
            b.readers = []
        self.n_ins += 1
        return ev

    def barrier(self):
        for eng in self.streams:
            waits = []
            kn = self.known[eng]
            for o in self.streams:
                if o == eng:
                    continue
                k = "eng_" + o
                if self.count[o] > kn.get(k, 0):
                    kn[k] = self.count[o]
                    waits.append((k, self.count[o]))
            if self.store_count > kn.get("store", 0):
                kn["store"] = self.store_count
                waits.append(("store", self.store_count))
            for b in self.dbufs:
                if b.dcount > kn.get(b.dsem, 0):
                    kn[b.dsem] = b.dcount
                    waits.append((b.dsem, b.dcount))
            if waits:
                self.streams[eng].append((waits, None, [], self.scope))

    def emit(self):
        nc = self.nc
        sems = self.sems
        engmap = {"pe": "tensor", "act": "scalar", "dve": "vector", "pool": "gpsimd", "sp": "sync"}

        def make(key):
            stream = self.streams[key]

            def one(e, waits, fn, incs):
                for k, v in waits:
                    e.wait_ge(sems[k], v)
                if fn is None:
                    return
                if isinstance(fn, tuple):
                    ins = getattr(e, fn[0])(**fn[1])
                else:
                    ins = fn(e)
                for k, v in incs:
                    ins = ins.then_inc(sems[k], v)

            def body(e):
                if not self.use_scopes:
                    for waits, fn, incs, _ in stream:
                        one(e, waits, fn, incs)
                    return
                i = 0
                while i < len(stream):
                    sc = stream[i][3]
                    j = i
                    while j < len(stream) and stream[j][3] == sc:
                        j += 1
                    if sc is None:
                        for waits, fn, incs, _ in stream[i:j]:
                            one(e, waits, fn, incs)
                    else:
                        with nc.named_scope(sc):
                            for waits, fn, incs, _ in stream[i:j]:
                                one(e, waits, fn, incs)
                    i = j
            return body
        with nc.Block() as block:
            for key, attr in engmap.items():
                if self.streams[key]:
                    getattr(block, attr)(make(key))


_UID = [0]


def _uid():
    _UID[0] += 1
    return _UID[0]


class Ring:
    def __init__(self, S, nc, stack, name, shape, dt, n, dma=False, psum=False):
        self.items = []
        for i in range(n):
            if psum:
                t = stack.enter_context(nc.psum_tensor(f"rp{_uid()}_{name}{i}", shape, dt))
            else:
                t = stack.enter_context(nc.sbuf_tensor(f"r{_uid()}_{name}{i}", shape, dt))
            self.items.append((t, S.buf(f"{name}{i}_{_uid()}", dma=dma)))
        self.i = 0

    def next(self):
        it = self.items[self.i % len(self.items)]
        self.i += 1
        return it


class Prog:
    def __init__(self, debug=False, stop_after=None, dbg_groups=None, dbg_slots=None, dbg_skip=(), scopes=False):
        self.debug = debug
        self.stop_after = stop_after
        self.dbg_groups = dbg_groups
        self.dbg_slots = dbg_slots
        self.dbg_skip = dbg_skip
        self.scopes = scopes
        self.nc = nc = bass.Bass("TRN2", target_bir_lowering=False)
        dk = "ExternalOutput" if debug else "Internal"

        def din(name, shape, dt=F32):
            return nc.dram_tensor(name, shape, dt, kind="ExternalInput").ap()

        self.xT_all = din("xT_all", [D, SEQ])
        self.xT_own = din("xT_own", [D, OWN])
        self.pos_all = din("pos_all", [1, SEQ], I32)
        self.pos_own = din("pos_own", [1, OWN], I32)
        self.memT = din("memT", [D, 256])
        self.w_in = din("w_in", [D, IN_W])
        self.w_uq = din("w_uq", [512, 1536])
        self.w_ukv = din("w_ukv", [256, 2048])
        self.w_mem = din("w_mem", [D, 2048])
        self.w_br = din("w_br", [3, 1024, D])
        self.w_out = din("w_out", [D, D])
        self.relb = din("relb", [1, 256])
        self.cst_d = din("cst", [128, NCST])
        self.outT = nc.dram_tensor("outT", [D, OWN], F32, kind="ExternalOutput").ap()

        def scr(name, shape, dt=BF16):
            return nc.dram_tensor(name, shape, dt, kind=dk).ap()

        self.s_kTA = scr("s_kTA", [2, 128, SEQ])
        self.s_VA = scr("s_VA", [SEQ, 256])
        self.s_kiT = scr("s_kiT", [128, SEQ])
        self.s_cT = scr("s_cT", [2, 128, SEQ])
        self.s_kpeR = scr("s_kpeR", [64, SEQ])
        self.s_sqpe = scr("s_sqpe", [64, SEQ])
        self.s_hT = scr("s_hT", [128, KC, OWN])
        self.s_silu = scr("s_silu", [3, 8, 128, OWN])
        self.s_u = scr("s_u", [3, 8, 128, OWN])
        self.s_qTA = scr("s_qTA", [128, 8, OWN])
        self.s_qiT = scr("s_qiT", [128, 8, OWN])
        if debug:
            self.dbg = {}

    def sb(self, stack, name, shape, dt=F32):
        return stack.enter_context(self.nc.sbuf_tensor(f"t{_uid()}_{name}", shape, dt))

    def gcol(self, c, rows=128):
        return self.cst[0:rows, c:c + 1]

    def build(self):
        nc = self.nc
        top = ExitStack()
        with top:
            self.top = top
            S = self.S = Sched(nc, top)
            S.use_scopes = self.scopes
            for n in ("kTA", "VA", "kiT", "cT", "kpeR", "sqpe", "hT", "out", "qTA", "qiT"):
                setattr(self, "B_" + n, S.buf(n))
            self.B_silu = [S.buf(f"silu{i}") for i in range(3)]
            self.B_u = [S.buf(f"u{i}") for i in range(3)]

            cst = self.cst = self.sb(top, "cst", [128, NCST])
            b_cst = self.b_cst = S.buf("cst", dma=True)
            S.dma(lambda e: e.dma_start(out=cst[:], in_=self.cst_d[:, :]), writes=[b_cst])
            ident = self.ident = self.sb(top, "ident", [128, 128], BF16)
            ident4 = self.ident4 = self.sb(top, "ident4", [128, 512], BF16)
            ones = self.ones = self.sb(top, "ones", [128, 128], BF16)
            b_const = self.b_const = S.buf("const")
            S.op("dve", lambda e: e.tensor_copy(out=ident[:], in_=cst[:, C_ID:C_ID + 128]), reads=[b_cst], writes=[b_const])
            for i in range(4):
                S.op("pool", lambda e, i=i: e.tensor_copy(out=ident4[:, i * 128:(i + 1) * 128], in_=cst[:, C_ID:C_ID + 128]),
                     reads=[b_cst], writes=[b_const])
            S.op("pool", lambda e: e.memset(ones[:], 1.0), writes=[b_const])
            self.psb = [top.enter_context(nc.psum_tensor(f"ps{i}", [128, 512], F32)) for i in range(8)]
            self.pbuf = [S.buf(f"ps{i}") for i in range(8)]

            phases = [("P0", self.phase0), ("P1", self.phase1), ("PA", self.phaseA), ("PB", self.phaseB),
                      ("PC", self.phaseC), ("PF", self.phaseF)]
            for name, fn in phases:
                if name in self.dbg_skip:
                    continue
                S.scope = name
                fn()
                S.barrier()
                if self.stop_after == name:
                    break
            S.barrier()
            S.emit()
        return nc

    def psring(self, idxs):
        prog = self

        class PsRing:
            def __init__(self):
                self.i = 0

            def next(self):
                k = idxs[self.i % len(idxs)]
                self.i += 1
                return prog.psb[k], prog.pbuf[k]
        return PsRing()

    def rsqrt_ps(self, ps_ap, psbuf, n_part, width, inv_n, out_ap, out_buf, tmp_ring):
        S = self.S
        tmp, tb = tmp_ring.next()
        S.op("act", ("activation", dict(out=tmp[0:n_part, 0:width], in_=ps_ap, func=AF.Sqrt, bias=self.gcol_eps(n_part), scale=inv_n)),
             reads=[psbuf, self.b_const], writes=[tb])
        S.op("dve", ("reciprocal", dict(out=out_ap, in_=tmp[0:n_part, 0:width])), reads=[tb], writes=[out_buf])

    def gcol_eps(self, rows=128):
        return self.epsc[0:rows, 0:1]

    def trig_alloc(self, stack, ncols, tag):
        S = self.S
        t = {}
        t["pi"] = Ring(S, self.nc, stack, f"tgpi{tag}", [64, ncols], I32, 2, dma=True)
        for n, dt in (("a", F32), ("u", F32), ("k", I32), ("kf", F32)):
            t[n] = (self.sb(stack, f"tg{n}{tag}", [64, ncols], dt), S.buf(f"tg{n}{tag}"))
        return t

    def trig(self, t, pos_ap, cs_tile, cs_buf):
        S = self.S
        cst, b_cst = self.cst, self.b_cst
        pi_t, b_pi = t["pi"].next()
        (a_t, b_a), (u_t, b_u), (k_t, b_k), (kf_t, b_kf) = t["a"], t["u"], t["k"], t["kf"]
        S.dma(lambda e: e.dma_start(out=pi_t[:], in_=pos_ap.partition_broadcast(64)), writes=[b_pi])
        S.op("dve", lambda e: e.tensor_copy(out=a_t[:], in_=pi_t[:]), reads=[b_pi], writes=[b_a])
        S.op("dve", lambda e: e.tensor_scalar(out=a_t[:], in0=a_t[:], scalar1=cst[0:64, C_INVF:C_INVF + 1], scalar2=None,
                                              op0=ALU.mult), reads=[b_a, b_cst], writes=[b_a])
        for idx, shift in ((1, 0.0), (0, 0.5 * math.pi)):
            S.op("dve", lambda e, shift=shift: e.tensor_scalar(out=u_t[:], in0=a_t[:], scalar1=shift, scalar2=None, op0=ALU.add),
                 reads=[b_a], writes=[b_u])
            S.op("dve", lambda e: e.tensor_scalar(out=k_t[:], in0=u_t[:], scalar1=1.0 / TWO_PI, scalar2=None, op0=ALU.mult),
                 reads=[b_u], writes=[b_k])
            S.op("dve", lambda e: e.tensor_copy(out=kf_t[:], in_=k_t[:]), reads=[b_k], writes=[b_kf])
            S.op("dve", lambda e: e.scalar_tensor_tensor(out=u_t[:], in0=kf_t[:], scalar=-CW1, in1=u_t[:], op0=ALU.mult, op1=ALU.add),
                 reads=[b_kf, b_u], writes=[b_u])
            S.op("dve", lambda e: e.scalar_tensor_tensor(out=u_t[:], in0=kf_t[:], scalar=-CW2, in1=u_t[:], op0=ALU.mult, op1=ALU.add),
                 reads=[b_kf, b_u], writes=[b_u])
            S.op("dve", lambda e: e.tensor_scalar(out=u_t[:], in0=u_t[:], scalar1=math.pi, scalar2=-math.pi, op0=ALU.min, op1=ALU.max),
                 reads=[b_u], writes=[b_u])
            S.op("act", lambda e, idx=idx: e.activation(out=cs_tile[:, idx, :], in_=u_t[:], func=AF.Sin), reads=[b_u], writes=[cs_buf])

    def wloader(self, stack, tag, nstage=2, pool_share=True):
        prog = self
        S = self.S

        class WLoader:
            def __init__(self):
                self.stage = Ring(S, prog.nc, stack, f"wst{tag}", [128, KC, 256], F32, nstage, dma=True)
                self.flip = 0

            def load(self, dst_tile, dst_buf, dram_ap, nk, ncols, gain_col=None, dst_c0=0):
                st, sbf = self.stage.next()
                src = dram_ap.rearrange("(k p) c -> p k c", p=128)
                S.dma(lambda e: e.dma_start(out=st[:, 0:nk, 0:ncols], in_=src), writes=[sbf])
                if gain_col is None:
                    h1 = (2 * nk + 2) // 3 if pool_share else nk
                    S.op("dve", ("tensor_copy", dict(out=dst_tile[:, 0:h1, dst_c0:dst_c0 + ncols], in_=st[:, 0:h1, 0:ncols])),
                         reads=[sbf], writes=[dst_buf])
                    if h1 < nk:
                        S.op("pool", ("tensor_copy", dict(out=dst_tile[:, h1:nk, dst_c0:dst_c0 + ncols], in_=st[:, h1:nk, 0:ncols])),
                             reads=[sbf], writes=[dst_buf])
                else:
                    for k in range(nk):
                        eng = ("dve", "dve", "pool")[self.flip % 3] if pool_share else "dve"
                        self.flip += 1
                        if eng == "dve":
                            S.op(eng, ("tensor_scalar", dict(out=dst_tile[:, k, dst_c0:dst_c0 + ncols], in0=st[:, k, 0:ncols],
                                                             scalar1=prog.gcol(gain_col + k), scalar2=None, op0=ALU.mult)),
                                 reads=[sbf, prog.b_cst], writes=[dst_buf])
                        else:
                            S.op(eng, ("tensor_scalar", dict(out=dst_tile[:, k, dst_c0:dst_c0 + ncols], in0=st[:, k, 0:ncols],
                                                             scalar1=prog.gcol(gain_col + k), scalar2=0.0, op0=ALU.mult, op1=ALU.add)),
                                 reads=[sbf, prog.b_cst], writes=[dst_buf])
        return WLoader()

    def normalize_tokens(self, xs, bxs, sq, bsq, ps, pb, rstd_ap, brstd, hT, bhT, tmp_ring):
        S = self.S
        ones = self.ones
        S.op("act", lambda e: e.activation(out=sq[:], in_=xs[:], func=AF.Square), reads=[bxs], writes=[bsq])
        for k in range(KC):
            S.op("pe", lambda e, k=k: e.matmul(ps[:, :], lhsT=ones[:, :], rhs=sq[:, k, :], start=(k == 0), stop=(k == KC - 1)),
                 reads=[bsq, self.b_const], writes=[pb])
        self.rsqrt_ps(ps[:, :], pb, 128, 512, 1.0 / D, rstd_ap, brstd, tmp_ring)
        S.op("dve", lambda e: e.tensor_tensor(out=hT[:, 0:10, :], in0=xs[:, 0:10, :],
                                              in1=rstd_ap.unsqueeze(1).to_broadcast([128, 10, 512]), op=ALU.mult),
             reads=[bxs, brstd], writes=[bhT])
        S.op("pool", lambda e: e.tensor_tensor(out=hT[:, 10:16, :], in0=xs[:, 10:16, :],
                                               in1=rstd_ap.unsqueeze(1).to_broadcast([128, 6, 512]), op=ALU.mult),
             reads=[bxs, brstd], writes=[bhT])

    def phase0(self):
        S, nc = self.S, self.nc
        self.epsc = self.sb(self.top, "epsc", [128, 1])
        S.op("pool", lambda e: e.memset(self.epsc[:], EPS), writes=[self.b_const])
        with ExitStack() as ph:
            self.rstd_own = self.sb(ph, "rstd_own", [128, OWN])
            self.b_rstd_own = S.buf("rstd_own")
            xs_ring = Ring(S, nc, ph, "p0xs", [128, KC, 512], F32, 1, dma=True)
            sq = self.sb(ph, "p0sq", [128, KC, 512], BF16)
            bsq = S.buf("p0sq")
            hT = self.sb(ph, "p0hT", [128, KC, 512], BF16)
            bhT = S.buf("p0hT")
            tmp_ring = Ring(S, nc, ph, "p0tmp", [128, 512], F32, 2)
            xsrc = self.xT_own.rearrange("(k p) t -> p k t", p=128)
            for tg in range(2):
                cols = slice(tg * 512, (tg + 1) * 512)
                xs, bxs = xs_ring.next()
                S.dma(lambda e, xs=xs, cols=cols: e.dma_start(out=xs[:], in_=xsrc[:, :, cols]), writes=[bxs])
                self.normalize_tokens(xs, bxs, sq, bsq, self.psb[tg], self.pbuf[tg], self.rstd_own[:, cols], self.b_rstd_own,
                                      hT, bhT, tmp_ring)
                S.dma(lambda e, cols=cols: e.dma_start(out=self.s_hT[:, :, cols], in_=hT[:]), reads=[bhT], writes=[self.B_hT], q="act")

    def phase1(self):
        S, nc = self.S, self.nc
        cst, b_cst, ones, b_const = self.cst, self.b_cst, self.ones, self.b_const
        w_in = self.w_in
        with ExitStack() as ph:
            Wk = self.sb(ph, "Wk", [128, KC, 960], BF16)
            b_Wk = S.buf("Wk")
            with ExitStack() as wls:
                wl = self.wloader(wls, "p1")
                wl.load(Wk, b_Wk, w_in[:, O_AK:O_AK + 256], KC, 256, G_NORM, 0)
                wl.load(Wk, b_Wk, w_in[:, O_IK:O_IK + 64], KC, 64, G_NORM, 256)
                wl.load(Wk, b_Wk, w_in[:, O_IK:O_IK + 64], KC, 64, G_NORM, 320)
                wl.load(Wk, b_Wk, w_in[:, O_BCKV:O_BCKV + 256], KC, 256, G_NORM, 384)
                wl.load(Wk, b_Wk, w_in[:, O_BKPE:O_BKPE + 64], KC, 64, G_NORM, 640)
                wl.load(Wk, b_Wk, w_in[:, O_AV:O_AV + 256], KC, 256, G_NORM, 704)
            S.barrier()
            pm32 = cst[0:64, C_PM:C_PM + 64]

            xs_ring = Ring(S, nc, ph, "p1xs", [128, KC, 512], F32, 2, dma=True)
            sq_ring = Ring(S, nc, ph, "p1sq", [128, KC, 512], BF16, 1)
            hT_ring = Ring(S, nc, ph, "p1hT", [128, KC, 512], BF16, 2)
            rstd_ring = Ring(S, nc, ph, "p1rstd", [128, 512], F32, 1)
            tmp_ring = Ring(S, nc, ph, "p1tmp", [128, 512], F32, 2)
            sqs_ring = Ring(S, nc, ph, "p1sqs", [128, 512], BF16, 3)
            rr_ring = Ring(S, nc, ph, "p1rr", [128, 512], F32, 2)
            o16_ring = Ring(S, nc, ph, "p1o16", [128, 512], BF16, 6)
            vo_ring = Ring(S, nc, ph, "p1vo", [128, 4, 256], BF16, 2)
            kg_ring = Ring(S, nc, ph, "p1kg", [64, 512], F32, 4)
            cs_ring = Ring(S, nc, ph, "p1cs", [64, 2, 512], F32, 3)
            trg = self.trig_alloc(ph, 512, "k")
            pr = self.psring([0, 1, 2, 3, 4, 5, 6, 7])
            xsrc = self.xT_all.rearrange("(k p) t -> p k t", p=128)
            ngroups = self.dbg_groups or SEQ // 512

            def p1_load(g):
                xs, bxs = xs_ring.next()
                S.dma(lambda e: e.dma_start(out=xs[:], in_=xsrc[:, :, g * 512:(g + 1) * 512]), writes=[bxs])
                return xs, bxs

            prt = self.psring([5, 6, 7])
            HB = [(self.psb[i], self.pbuf[i]) for i in range(5)]

            def pre(g):
                cols = slice(g * 512, (g + 1) * 512)
                xs, bxs = p1_load(g)
                cs, bcs = cs_ring.next()
                self.trig(trg, self.pos_all[0:1, cols], cs, bcs)
                sq, bsq = sq_ring.next()
                ps, pb = prt.next()
                rstd, brstd = rstd_ring.next()
                hT, bhT = hT_ring.next()
                self.normalize_tokens(xs, bxs, sq, bsq, ps, pb, rstd[:, :], brstd, hT, bhT, tmp_ring)
                return hT, bhT, cs, bcs

            def fm(hT, bhT, c0, m, ps, pb):
                for k in range(KC):
                    self.MM(ps[0:m, :], Wk[:, k, c0:c0 + m], hT[:, k, :], k == 0, k == KC - 1, [b_Wk, bhT], [pb])

            def norm_store(pss, gain_cols, inv_n, dsts):
                sqs = []
                for ps, pb in pss:
                    s16, bs16 = sqs_ring.next()
                    self.OP("act", "activation", [pb], [bs16], out=s16[:, :], in_=ps[:, :], func=AF.Square)
                    sqs.append((s16, bs16))
                pz, pzb = prt.next()
                for i, (s16, bs16) in enumerate(sqs):
                    self.MM(pz[:, :], ones[:, :], s16[:, :], i == 0, i == len(sqs) - 1, [bs16, b_const], [pzb])
                rr, brr = rr_ring.next()
                self.rsqrt_ps(pz[:, :], pzb, 128, 512, inv_n, rr[:, :], brr, tmp_ring)
                for i, (ps, pb) in enumerate(pss):
                    o16, bo = o16_ring.next()
                    self.OP("dve", "scalar_tensor_tensor", [pb, brr, b_cst], [bo], out=o16[:, :], in0=ps[:, :],
                            scalar=self.gcol(gain_cols[i]), in1=rr[:, :], op0=ALU.mult, op1=ALU.mult)
                    dst_ap, dst_buf = dsts[i]
                    self.DMA(dst_ap, o16[:, :], [bo], [dst_buf])

            def proj(g, hT, bhT):
                cols = slice(g * 512, (g + 1) * 512)
                fm(hT, bhT, 0, 128, *HB[0])
                fm(hT, bhT, 128, 128, *HB[1])
                fm(hT, bhT, 384, 128, *HB[2])
                fm(hT, bhT, 512, 128, *HB[3])
                fm(hT, bhT, 640, 64, *HB[4])
                ps, pb = prt.next()
                fm(hT, bhT, 256, 128, ps, pb)
                o16, bo = o16_ring.next()
                self.OP("act", "activation", [pb], [bo], out=o16[:, :], in_=ps[:, :], func=AF.Copy)
                self.DMA(self.s_kiT[:, cols], o16[:, :], [bo], [self.B_kiT])
                vo, bvo = vo_ring.next()
                for tb in range(4):
                    ps, pb = prt.next()
                    for k in range(KC):
                        self.MM(ps[:, 0:256], hT[:, k, tb * 128:(tb + 1) * 128], Wk[:, k, 704:960], k == 0, k == KC - 1, [b_Wk, bhT], [pb])
                    self.OP("act", "activation", [pb], [bvo], out=vo[:, tb, :], in_=ps[:, 0:256], func=AF.Copy)
                self.DMA(self.s_VA[g * 512:(g + 1) * 512, :].rearrange("(b p) c -> p b c", p=128), vo[:], [bvo], [self.B_VA])

            def post(g, cs, bcs):
                cols = slice(g * 512, (g + 1) * 512)
                for kvh in range(2):
                    norm_store([HB[kvh]], [G_AK], 1.0 / 128, [(self.s_kTA[kvh, :, cols], self.B_kTA)])
                norm_store([HB[2], HB[3]], [G_KVLAT, G_KVLAT + 1], 1.0 / 256,
                           [(self.s_cT[0, :, cols], self.B_cT), (self.s_cT[1, :, cols], self.B_cT)])
                ps, pb = HB[4]
                s16, bs16 = sqs_ring.next()
                self.OP("act", "activation", [pb], [bs16], out=s16[0:64, :], in_=ps[0:64, :], func=AF.Square)
                self.DMA(self.s_sqpe[:, cols], s16[0:64, :], [bs16], [self.B_sqpe])
                kg, bkg = kg_ring.next()
                self.OP("dve", "tensor_scalar", [pb, b_cst], [bkg], out=kg[:, :], in0=ps[0:64, :], scalar1=self.gcol(G_BKR, 64),
                        scalar2=None, op0=ALU.mult)
                px, pxb = prt.next()
                self.MM(px[0:64, :], pm32, kg[:, :], True, True, [bkg, b_cst], [pxb])
                t1, bt1 = kg_ring.next()
                self.OP("pool", "tensor_tensor", [bkg, bcs], [bt1], out=t1[:, :], in0=kg[:, :], in1=cs[:, 0, :], op=ALU.mult)
                t2, bt2 = kg_ring.next()
                self.OP("dve", "tensor_tensor", [pxb, bcs], [bt2], out=t2[:, :], in0=px[0:64, :], in1=cs[:, 1, :], op=ALU.mult)
                o16, bo = o16_ring.next()
                self.OP("dve", "tensor_tensor", [bt1, bt2], [bo], out=o16[0:64, :], in0=t1[:, :], in1=t2[:, :], op=ALU.add)
                self.DMA(self.s_kpeR[:, cols], o16[0:64, :], [bo], [self.B_kpeR])

            cur = pre(0)
            for g in range(ngroups):
                nxt = pre(g + 1) if g + 1 < ngroups else None
                proj(g, cur[0], cur[1])
                post(g, cur[2], cur[3])
                cur = nxt
    def OP(self, eng, m, reads, writes, **kw):
        return self.S.op(eng, (m, kw), reads, writes)

    def DMA(self, out, in_, reads, writes):
        is_store = not any(b.dsem is not None for b in writes)
        return self.S.dma(("dma_start", dict(out=out, in_=in_)), reads, writes, q=("act" if is_store else "sp"))

    def MM(self, out, lhsT, rhs, start, stop, reads, writes, **extra):
        return self.S.op("pe", ("matmul", dict(out=out, lhsT=lhsT, rhs=rhs, start=start, stop=stop, **extra)), reads, writes)

    def load_hT_own(self, stack):
        hT = self.sb(stack, "hTown", [128, KC, OWN], BF16)
        b = self.S.buf(f"hTown{_uid()}", dma=True)
        for h2 in range(2):
            self.DMA(hT[:, h2 * 8:(h2 + 1) * 8, :], self.s_hT[:, h2 * 8:(h2 + 1) * 8, :], [self.B_hT], [b])
        return hT, b

    def own_proj(self, wl, wring, hT, b_hT, pr, specs, post):
        def load(i):
            ap, ncols, gain, _ = specs[i]
            wt, wb = wring.next()
            wl.load(wt, wb, ap, KC, ncols, gain)
            return wt, wb
        nxt = load(0)
        for i, spec in enumerate(specs):
            wt, wb = nxt
            if i + 1 < len(specs):
                nxt = load(i + 1)
            for (c0, m, tag, idx) in spec[3]:
                for tg in range(2):
                    ps, pb = pr.next()
                    for k in range(KC):
                        self.MM(ps[0:m, :], wt[:, k, c0:c0 + m], hT[:, k, tg * 512:(tg + 1) * 512], k == 0, k == KC - 1,
                                [wb, b_hT], [pb])
                    post(tag, idx, tg, ps, pb)

    def phaseA(self):
        S, nc = self.S, self.nc
        cst, b_cst, ones, b_const, ident, ident4 = self.cst, self.b_cst, self.ones, self.b_const, self.ident, self.ident4
        w_in = self.w_in
        nslots = self.dbg_slots or NSLOT
        with ExitStack() as pa:
            iw_abs = self.sb(pa, "iw_abs", [128, NSLOT, 16])
            iw_sgn = self.sb(pa, "iw_sgn", [128, NSLOT, 16])
            biasSel = self.sb(pa, "biasSel", [128, 9, 8, 128], BF16)
            gsc = self.sb(pa, "gscA", [128, 1])
            b_qTA, b_qiT, b_iw, b_bias, b_gsc = (S.buf(n) for n in ("qTA", "qiT", "iw", "biasSel", "gscA"))
            self.OP("dve", "tensor_scalar", [b_cst], [b_gsc], out=gsc[:, :], in0=self.gcol(G_AQ), scalar1=A_SCALE, scalar2=None,
                    op0=ALU.mult)
            with ExitStack() as sp1:
                hT, b_hT = self.load_hT_own(sp1)
                qTA = self.sb(sp1, "qTA", [128, 8, OWN], BF16)
                qiT = self.sb(sp1, "qiT", [128, 8, OWN], BF16)
                wl = self.wloader(sp1, "pa")
                wring = Ring(S, nc, sp1, "pawr", [128, KC, 256], BF16, 2)
                sqs_ring = Ring(S, nc, sp1, "pasqs", [128, 512], BF16, 2)
                rr_ring = Ring(S, nc, sp1, "parr", [128, 512], F32, 2)
                tmp_ring = Ring(S, nc, sp1, "patmp", [128, 512], F32, 2)
                so_ring = Ring(S, nc, sp1, "paso", [128, 512], BF16, 2)
                pr = self.psring([0, 1, 2, 3, 4, 5])
                prz = self.psring([6, 7])

                def post(tag, idx, tg, ps, pb):
                    cols = slice(tg * 512, (tg + 1) * 512)
                    if tag == "aq":
                        s16, bs = sqs_ring.next()
                        self.OP("act", "activation", [pb], [bs], out=s16[:, :], in_=ps[:, :], func=AF.Square)
                        pz, pzb = prz.next()
                        self.MM(pz[:, :], ones[:, :], s16[:, :], True, True, [bs, b_const], [pzb])
                        rr, brr = rr_ring.next()
                        self.rsqrt_ps(pz[:, :], pzb, 128, 512, 1.0 / 128, rr[:, :], brr, tmp_ring)
                        self.OP("dve", "scalar_tensor_tensor", [pb, brr, b_gsc], [b_qTA], out=qTA[:, idx, cols], in0=ps[:, :],
                                scalar=gsc[:, 0:1], in1=rr[:, :], op0=ALU.mult, op1=ALU.mult)
                    elif tag == "iq":
                        self.OP("act", "activation", [pb], [b_qiT], out=qiT[:, idx, cols], in_=ps[:, :], func=AF.Copy)
                    elif tag == "az":
                        so, bso = so_ring.next()
                        self.OP("act", "activation", [pb], [bso], out=so[:, :], in_=ps[:, :], func=AF.Silu)
                        self.DMA(self.s_silu[0, idx, :, cols], so[:, :], [bso], [self.B_silu[0]])

                specs = []
                for tag, o in (("aq", O_AQ), ("iq", O_IQ), ("az", O_AZ)):
                    for i in range(4):
                        specs.append((w_in[:, o + 256 * i:o + 256 * (i + 1)], 256, G_NORM,
                                      [(0, 128, tag, 2 * i), (128, 128, tag, 2 * i + 1)]))
                self.own_proj(wl, wring, hT, b_hT, pr, specs, post)
                wt, wb = wring.next()
                wl.load(wt, wb, w_in[:, O_IW:O_IW + 16], KC, 16, G_NORM)
                for tb in range(NSLOT):
                    ps, pb = pr.next()
                    for k in range(KC):
                        self.MM(ps[:, 0:16], hT[:, k, tb * 128:(tb + 1) * 128], wt[:, k, 0:16], k == 0, k == KC - 1, [wb, b_hT], [pb])
                    self.OP("act", "activation", [pb], [b_iw], out=iw_abs[:, tb, :], in_=ps[:, 0:16], func=AF.Abs)
                    self.OP("act", "activation", [pb], [b_iw], out=iw_sgn[:, tb, :], in_=ps[:, 0:16], func=AF.Sign)
                tabbc = self.sb(sp1, "tabbc", [128, 256])
                b_tab = S.buf("tabbc", dma=True)
                self.DMA(tabbc[:, :], self.relb[0:1, :].partition_broadcast(128), [], [b_tab])
                Bacc = self.sb(sp1, "Bacc", [128, 8, 256])
                eq = self.sb(sp1, "eqb", [128, 256])
                tmpB = self.sb(sp1, "tmpB", [128, 8, 128])
                b_Bacc, b_eq, b_tmpB = S.buf("Bacc"), S.buf("eqb"), S.buf("tmpB")
                bkt = cst[:, C_BK:C_BK + 256]
                tabd = self.sb(sp1, "tabd", [128, 32, 8])
                self.OP("dve", "tensor_tensor", [b_tab], [b_tab], out=tabd[:, :, :], in0=tabbc[:, :].rearrange("p (b h) -> p b h", h=8),
                        in1=tabbc[:, 248:256].unsqueeze(1).to_broadcast([128, 32, 8]), op=ALU.subtract)
                self.OP("dve", "memset", [], [b_Bacc], ap=Bacc[:, :, :], constant=0.0)
                for b in range(31):
                    self.OP("dve", "tensor_scalar", [b_cst], [b_eq], out=eq[:, :], in0=bkt, scalar1=float(b), scalar2=None,
                            op0=ALU.is_equal)
                    for h in range(8):
                        self.OP("dve", "scalar_tensor_tensor", [b_eq, b_tab, b_Bacc], [b_Bacc], out=Bacc[:, h, :], in0=eq[:, :],
                                scalar=tabd[:, b, h:h + 1], in1=Bacc[:, h, :], op0=ALU.mult, op1=ALU.add)
                for r in range(9):
                    self.OP("dve", "tensor_scalar", [b_Bacc, b_cst], [b_tmpB], out=tmpB[:, :, :], in0=Bacc[:, :, 128:256],
                            scalar1=cst[:, C_SEL + 9 + r:C_SEL + 10 + r], scalar2=None, op0=ALU.mult)
                    self.OP("dve", "scalar_tensor_tensor", [b_Bacc, b_cst, b_tmpB], [b_bias], out=biasSel[:, r, :, :],
                            in0=Bacc[:, :, 0:128], scalar=cst[:, C_SEL + r:C_SEL + r + 1], in1=tmpB[:, :, :],
                            op0=ALU.mult, op1=ALU.add)

                for h2 in range(2):
                    self.DMA(self.s_qTA[:, h2 * 4:(h2 + 1) * 4, :], qTA[:, h2 * 4:(h2 + 1) * 4, :], [b_qTA], [self.B_qTA])
                    self.DMA(self.s_qiT[:, h2 * 4:(h2 + 1) * 4, :], qiT[:, h2 * 4:(h2 + 1) * 4, :], [b_qiT], [self.B_qiT])
            S.barrier()
            if self.debug:
                self.dbg_store("iw_abs", iw_abs, [128, NSLOT, 16], F32, [b_iw])
                self.dbg_store("iw_sgn", iw_sgn, [128, NSLOT, 16], F32, [b_iw])
                self.dbg_store("biasSel", biasSel, [128, 9, 8, 128], BF16, [b_bias])

            S.scope = "PA2load"
            with ExitStack() as sp2:
                kTA = self.sb(sp2, "kTA", [128, 2, SEQ], BF16)
                vA = self.sb(sp2, "vA", [128, NB, 256], BF16)
                kiT = self.sb(sp2, "kiT", [128, SEQ], BF16)
                b_kTA, b_vA, b_kiT = S.buf("kTAs", dma=True), S.buf("vAs", dma=True), S.buf("kiTs", dma=True)
                self.DMA(kiT[:, :], self.s_kiT[:, :], [self.B_kiT], [b_kiT])
                for g in range(2):
                    self.DMA(kTA[:, g, :], self.s_kTA[g, :, :], [self.B_kTA], [b_kTA])
                vsrc = self.s_VA.rearrange("(b p) c -> p b c", p=128)
                for q4 in range(4):
                    self.DMA(vA[:, q4 * 16:(q4 + 1) * 16, :], vsrc[:, q4 * 16:(q4 + 1) * 16, :], [self.B_VA], [b_vA])
                score = self.sb(sp2, "score", [128, SEQ])
                b_score = [S.buf(f"score{i}") for i in range(SEQ // 512)]
                mask_ring = Ring(S, nc, sp2, "maskadd", [128, SEQ], BF16, 2)
                qa_ring = Ring(S, nc, sp2, "paqa", [128, 8, 128], BF16, 2, dma=True)
                qi_ring = Ring(S, nc, sp2, "paqi", [128, 8, 128], BF16, 2, dma=True)
                junk = self.sb(sp2, "junk", [128, 4096], BF16)
                b_junk = S.buf("junk")
                R_ring = Ring(S, nc, sp2, "paR", [128, 512], F32, 2)
                PT_ring = Ring(S, nc, sp2, "paPT", [128, 512], BF16, 3)
                rz_ring = Ring(S, nc, sp2, "parz", [128, 512], F32, 2)
                uo_ring = Ring(S, nc, sp2, "pauo", [128, 4, 128], BF16, 2)
                sz_ring = Ring(S, nc, sp2, "pasz", [128, 8, 128], BF16, 1, dma=True)
                sm = self.sb(sp2, "pasm", [128, 64])
                b_lo = [S.buf("lo0"), S.buf("lo1")]
                b_hi, b_w0, b_mid, b_cnt, b_gew, b_cnts, b_W = (S.buf(n) for n in ("hi", "w0", "mid", "cnt", "gew", "cnts", "W"))
                Wc = 16
                prw = self.psring([0, 1, 2, 3])

                slot = {}

                def idx(j):
                    qcols = slice(j * 128, (j + 1) * 128)
                    L = 1024 * (j + 1)
                    nch = L // 512
                    qi, b_qi = qi_ring.next()
                    qa, b_qa = qa_ring.next()
                    self.DMA(qi[:, :, :], self.s_qiT[:, :, qcols], [self.B_qiT], [b_qi])
                    self.DMA(qa[:, :, :], self.s_qTA[:, :, qcols], [self.B_qTA], [b_qa])
                    maskadd, b_mask = mask_ring.next()
                    slot[j] = dict(qa=qa, b_qa=b_qa, maskadd=maskadd, b_mask=b_mask)
                    S.scope = f"PAidx{j}"
                    for c5 in range(nch):
                        kcols = slice(c5 * 512, (c5 + 1) * 512)
                        for h in range(16):
                            hp, half = h // 2, h % 2
                            prt = slice(half * 64, half * 64 + 64)
                            ps, pb = prw.next()
                            self.MM(ps[:, :], qi[prt, hp, :], kiT[prt, kcols], True, True, [b_qi, b_kiT], [pb])
                            R, bR = R_ring.next()
                            self.OP("act", "activation", [pb, b_iw], [bR], out=R[:, :], in_=ps[:, :], func=AF.Relu,
                                    scale=iw_abs[:, j, h:h + 1])
                            if h == 0:
                                self.OP("dve", "tensor_scalar", [bR, b_iw], [b_score[c5]], out=score[:, kcols], in0=R[:, :],
                                        scalar1=iw_sgn[:, j, 0:1], scalar2=None, op0=ALU.mult)
                            else:
                                self.OP("dve", "scalar_tensor_tensor", [bR, b_iw, b_score[c5]], [b_score[c5]], out=score[:, kcols],
                                        in0=R[:, :], scalar=iw_sgn[:, j, h:h + 1], in1=score[:, kcols], op0=ALU.mult, op1=ALU.add)

                def bis(j):
                    L = 1024 * (j + 1)
                    nch = L // 512
                    maskadd, b_mask = slot[j]["maskadd"], slot[j]["b_mask"]
                    sbufs = b_score[0:nch]
                    S.scope = f"PAbis{j}"
                    self.OP("dve", "tensor_reduce", sbufs, [b_lo[0]], out=sm[:, 0:1], in_=score[:, 0:L], axis=AX.X, op=ALU.min)
                    for c5 in (nch - 2, nch - 1):
                        off = (c5 - (nch - 2)) * 512
                        self.OP("dve", "scalar_tensor_tensor", [b_cst, b_score[c5]], [b_score[c5]],
                                out=score[:, c5 * 512:(c5 + 1) * 512], in0=cst[:, C_CM + off:C_CM + off + 512], scalar=-1e30,
                                in1=score[:, c5 * 512:(c5 + 1) * 512], op0=ALU.mult, op1=ALU.add)
                    self.OP("dve", "tensor_reduce", sbufs, [b_hi], out=sm[:, 2:3], in_=score[:, 0:L], axis=AX.X, op=ALU.max)
                    self.OP("dve", "tensor_tensor", [b_hi, b_lo[0]], [b_w0], out=sm[:, 3:4], in0=sm[:, 2:3], in1=sm[:, 0:1],
                            op=ALU.subtract)
                    self.OP("dve", "tensor_scalar", [b_cst, b_w0], [b_W], out=sm[:, Wc:Wc + NITER], in0=cst[:, C_PW:C_PW + NITER],
                            scalar1=sm[:, 3:4], scalar2=None, op0=ALU.mult)
                    self.OP("dve", "tensor_tensor", [b_lo[0], b_W], [b_mid], out=sm[:, 4:5], in0=sm[:, 0:1], in1=sm[:, Wc:Wc + 1],
                            op=ALU.add)
                    npc = (L + 4095) // 4096
                    cur = 0
                    for it in range(NITER):
                        for p in range(npc):
                            c0, c1 = p * 4096, min(L, (p + 1) * 4096)
                            self.OP("dve", "tensor_scalar", b_score[c0 // 512:c1 // 512] + [b_mid], [b_junk, b_cnts],
                                    out=junk[:, 0:c1 - c0], in0=score[:, c0:c1], scalar1=sm[:, 4:5], scalar2=None,
                                    op0=ALU.is_ge, op1=ALU.add, accum_out=sm[:, 8 + p:9 + p])
                        if npc > 1:
                            self.OP("dve", "tensor_reduce", [b_cnts], [b_cnt], out=sm[:, 5:6], in_=sm[:, 8:8 + npc], axis=AX.X,
                                    op=ALU.add)
                            csrc, bcs = sm[:, 5:6], b_cnt
                        else:
                            csrc, bcs = sm[:, 8:9], b_cnts
                        self.OP("dve", "tensor_scalar", [bcs, b_W], [b_gew], out=sm[:, 6:7], in0=csrc, scalar1=TOPK - 0.5,
                                scalar2=sm[:, Wc + it:Wc + it + 1], op0=ALU.is_ge, op1=ALU.mult)
                        nxt = 1 - cur
                        if it + 1 < NITER:
                            self.OP("dve", "scalar_tensor_tensor", [b_gew, b_W, b_lo[cur]], [b_mid], out=sm[:, 4:5], in0=sm[:, 6:7],
                                    scalar=sm[:, Wc + it + 1:Wc + it + 2], in1=sm[:, cur:cur + 1], op0=ALU.add, op1=ALU.add)
                        self.OP("dve", "tensor_tensor", [b_gew, b_lo[cur]], [b_lo[nxt]], out=sm[:, nxt:nxt + 1], in0=sm[:, 6:7],
                                in1=sm[:, cur:cur + 1], op=ALU.add)
                        cur = nxt
                    self.OP("dve", "tensor_scalar", sbufs + [b_lo[cur]], [b_mask], out=maskadd[:, 0:L], in0=score[:, 0:L],
                            scalar1=sm[:, cur:cur + 1], scalar2=NEG, op0=ALU.is_lt, op1=ALU.mult)
                    if self.debug:
                        self.dbg_store(f"thr{j}", sm[:, cur:cur + 1], [128, 1], F32, [b_lo[cur]], raw_ap=True)

                def att(j):
                    qcols = slice(j * 128, (j + 1) * 128)
                    qa, b_qa = slot[j]["qa"], slot[j]["b_qa"]
                    maskadd, b_mask = slot[j]["maskadd"], slot[j]["b_mask"]
                    S.scope = f"PAatt{j}"
                    sz, bsz = sz_ring.next()
                    self.DMA(sz[:, :, :], self.s_silu[0, :, :, qcols].rearrange("h p t -> p h t"), [self.B_silu[0]], [bsz])
                    nkb = 8 * (j + 1)
                    psO = [(self.psb[4], self.pbuf[4]), (self.psb[5], self.pbuf[5])]
                    psZ = [(self.psb[6], self.pbuf[6]), (self.psb[7], self.pbuf[7])]
                    units = [(kb, g) for kb in range(nkb) for g in range(2)]
                    st = {}

                    def qk(u):
                        kb, g = units[u]
                        ks = slice(kb * 128, (kb + 1) * 128)
                        r = kb - (8 * j - 1)
                        near = 0 <= r <= 8
                        ps, pb = prw.next()
                        self.MM(ps[:, :], kTA[:, g, ks], qa[:, 4 * g:4 * g + 4, :], True, False, [b_kTA, b_qa], [pb])
                        self.MM(ps[:, :], maskadd[:, ks], ident4[:, :], False, not near, [b_mask, b_const], [pb])
                        if near:
                            self.MM(ps[:, :], ident[:, :], biasSel[:, r, 4 * g:4 * g + 4, :], False, True, [b_bias, b_const], [pb])
                        st[u] = (ps, pb)

                    def ex_pv(u):
                        kb, g = units[u]
                        ps, pb = st.pop(u)
                        PT, bPT = PT_ring.next()
                        self.OP("act", "activation", [pb], [bPT], out=PT[:, :], in_=ps[:, :], func=AF.Exp)
                        self.MM(psO[g][0][:, :], vA[:, kb, g * 128:(g + 1) * 128], PT[:, :], kb == 0, kb == nkb - 1,
                                [b_vA, bPT], [psO[g][1]])
                        self.MM(psZ[g][0][:, :], ones[:, :], PT[:, :], kb == 0, kb == nkb - 1, [b_const, bPT], [psZ[g][1]])
                    LOOK = 3
                    for u in range(min(LOOK, len(units))):
                        qk(u)
                    for u in range(len(units)):
                        if u + LOOK < len(units):
                            qk(u + LOOK)
                        ex_pv(u)
                    for g in range(2):
                        rz, brz = rz_ring.next()
                        self.OP("dve", "reciprocal", [psZ[g][1]], [brz], out=rz[:, :], in_=psZ[g][0][:, :])
                        rz2, brz2 = rz_ring.next()
                        self.OP("dve", "tensor_tensor", [psO[g][1], brz], [brz2], out=rz2[:, :], in0=psO[g][0][:, :], in1=rz[:, :],
                                op=ALU.mult)
                        uo, buo = uo_ring.next()
                        self.OP("pool", "tensor_tensor", [brz2, bsz], [buo], out=uo[:, :, :],
                                in0=rz2[:, :].rearrange("p (h t) -> p h t", h=4), in1=sz[:, 4 * g:4 * g + 4, :], op=ALU.mult)
                        self.DMA(self.s_u[0, 4 * g:4 * g + 4, :, qcols].rearrange("h p t -> p h t"), uo[:, :, :], [buo], [self.B_u[0]])


                idx(0)
                bis(0)
                for j in range(nslots):
                    if j + 1 < nslots:
                        idx(j + 1)
                        bis(j + 1)
                    att(j)

    def dbg_store(self, name, tile, shape, dt, reads, raw_ap=False):
        d = self.nc.dram_tensor("dbg_" + name, shape, dt, kind="ExternalOutput").ap()
        src = tile if raw_ap else tile[:]
        idx = tuple(slice(None) for _ in shape)
        self.DMA(d[idx], src, reads, [self.S.buf("dbg_" + name)])

    def phaseB(self):
        S, nc = self.S, self.nc
        cst, b_cst, ones, b_const, ident = self.cst, self.b_cst, self.ones, self.b_const, self.ident
        w_in = self.w_in
        nheads = self.dbg_slots or 8
        with ExitStack() as pbs:
            qNT = self.sb(pbs, "qNT", [128, 8, OWN], BF16)
            qRT = self.sb(pbs, "qRT", [64, 8, OWN], BF16)
            b_qNT, b_qRT = S.buf("qNT"), S.buf("qRT")
            gs = self.sb(pbs, "gscB", [128, 2])
            b_gs = S.buf("gscB")
            self.OP("dve", "scalar_tensor_tensor", [b_cst], [b_gs], out=gs[:, 0:1], in0=self.gcol(G_BQN), scalar=B_SCALE,
                    in1=self.gcol(G_BKN), op0=ALU.mult, op1=ALU.mult)
            self.OP("dve", "tensor_scalar", [b_cst], [b_gs], out=gs[0:64, 1:2], in0=self.gcol(G_BQR, 64), scalar1=B_SCALE, scalar2=None,
                    op0=ALU.mult)
            with ExitStack() as sp1:
                hT, b_hT = self.load_hT_own(sp1)
                wl = self.wloader(sp1, "pb")
                wring = Ring(S, nc, sp1, "pbwr", [128, KC, 256], BF16, 2)
                cs_own = self.sb(sp1, "cs_own", [64, 2, OWN])
                b_cs = S.buf("cs_own")
                trg = self.trig_alloc(sp1, OWN, "o")
                self.trig(trg, self.pos_own[0:1, :], cs_own, b_cs)
                cqT = self.sb(sp1, "cqT", [128, 4, OWN], BF16)
                b_cqT = S.buf("cqT")
                rq_lat = self.sb(sp1, "rq_lat", [128, OWN])
                b_rql = S.buf("rq_lat")
                wuq = self.sb(sp1, "wuq", [128, 4, 1536], BF16)
                b_wuq = S.buf("wuq")
                sqs_ring = Ring(S, nc, sp1, "pbsqs", [128, 512], BF16, 3)
                tmp_ring = Ring(S, nc, sp1, "pbtmp", [128, 512], F32, 2)
                so_ring = Ring(S, nc, sp1, "pbso", [128, 512], BF16, 2)
                qn_ring = Ring(S, nc, sp1, "pbqn", [128, 512], F32, 2)
                qr_ring = Ring(S, nc, sp1, "pbqr", [64, 512], F32, 4)
                rr_ring = Ring(S, nc, sp1, "pbrr", [128, 512], F32, 2)
                pr = self.psring([0, 1, 2, 3, 4, 5])
                pz = [(self.psb[6], self.pbuf[6]), (self.psb[7], self.pbuf[7])]
                for i in range(6):
                    wl.load(wuq, b_wuq, self.w_uq[:, 256 * i:256 * (i + 1)], 4, 256, G_QLAT, 256 * i)

                def post(tag, idx, tg, ps, pb):
                    cols = slice(tg * 512, (tg + 1) * 512)
                    if tag == "bcq":
                        self.OP("act", "activation", [pb], [b_cqT], out=cqT[:, idx, cols], in_=ps[:, :], func=AF.Copy)
                        s16, bs = sqs_ring.next()
                        self.OP("act", "activation", [pb], [bs], out=s16[:, :], in_=ps[:, :], func=AF.Square)
                        self.MM(pz[tg][0][:, :], ones[:, :], s16[:, :], idx == 0, idx == 3, [bs, b_const], [pz[tg][1]])
                        if idx == 3:
                            self.rsqrt_ps(pz[tg][0][:, :], pz[tg][1], 128, 512, 1.0 / 512, rq_lat[:, cols], b_rql, tmp_ring)
                    elif tag == "bz":
                        so, bso = so_ring.next()
                        self.OP("act", "activation", [pb], [bso], out=so[:, :], in_=ps[:, :], func=AF.Silu)
                        self.DMA(self.s_silu[1, idx, :, cols], so[:, :], [bso], [self.B_silu[1]])

                specs = []
                for i in range(2):
                    specs.append((w_in[:, O_BCQ + 256 * i:O_BCQ + 256 * (i + 1)], 256, G_NORM,
                                  [(0, 128, "bcq", 2 * i), (128, 128, "bcq", 2 * i + 1)]))
                for i in range(4):
                    specs.append((w_in[:, O_BZ + 256 * i:O_BZ + 256 * (i + 1)], 256, G_NORM,
                                  [(0, 128, "bz", 2 * i), (128, 128, "bz", 2 * i + 1)]))
                self.own_proj(wl, wring, hT, b_hT, pr, specs, post)
                pm32 = cst[0:64, C_PM:C_PM + 64]
                for h in range(8):
                    for tg in range(2):
                        cols = slice(tg * 512, (tg + 1) * 512)
                        psN, pbN = pr.next()
                        psR, pbR = pr.next()
                        for k in range(4):
                            self.MM(psN[:, :], wuq[:, k, h * 192:h * 192 + 128], cqT[:, k, cols], k == 0, k == 3, [b_wuq, b_cqT], [pbN])
                        for k in range(4):
                            self.MM(psR[0:64, :], wuq[:, k, h * 192 + 128:h * 192 + 192], cqT[:, k, cols], k == 0, k == 3,
                                    [b_wuq, b_cqT], [pbR])
                        qn, bqn = qn_ring.next()
                        qr, bqr = qr_ring.next()
                        self.OP("dve", "tensor_tensor", [pbN, b_rql], [bqn], out=qn[:, :], in0=psN[:, :], in1=rq_lat[:, cols], op=ALU.mult)
                        self.OP("dve", "tensor_tensor", [pbR, b_rql], [bqr], out=qr[:, :], in0=psR[0:64, :], in1=rq_lat[0:64, cols],
                                op=ALU.mult)
                        s1, bs1 = sqs_ring.next()
                        s2, bs2 = sqs_ring.next()
                        self.OP("act", "activation", [bqn], [bs1], out=s1[:, :], in_=qn[:, :], func=AF.Square)
                        self.OP("act", "activation", [bqr], [bs2], out=s2[0:64, :], in_=qr[:, :], func=AF.Square)
                        pq, pqb = pr.next()
                        self.MM(pq[:, :], ones[:, :], s1[:, :], True, False, [bs1, b_const], [pqb])
                        self.MM(pq[:, :], ones[0:64, :], s2[0:64, :], False, True, [bs2, b_const], [pqb])
                        rr, brr = rr_ring.next()
                        self.rsqrt_ps(pq[:, :], pqb, 128, 512, 1.0 / 192, rr[:, :], brr, tmp_ring)
                        self.OP("dve", "scalar_tensor_tensor", [bqn, brr, b_gs], [b_qNT], out=qNT[:, h, cols], in0=qn[:, :],
                                scalar=gs[:, 0:1], in1=rr[:, :], op0=ALU.mult, op1=ALU.mult)
                        qg, bqg = qr_ring.next()
                        self.OP("dve", "scalar_tensor_tensor", [bqr, brr, b_gs], [bqg], out=qg[:, :], in0=qr[:, :],
                                scalar=gs[0:64, 1:2], in1=rr[0:64, :], op0=ALU.mult, op1=ALU.mult)
                        px, pxb = pr.next()
                        self.MM(px[0:64, :], pm32, qg[:, :], True, True, [bqg, b_cst], [pxb])
                        t1, bt1 = qr_ring.next()
                        self.OP("pool", "tensor_tensor", [bqg, b_cs], [bt1], out=t1[:, :], in0=qg[:, :], in1=cs_own[:, 0, cols], op=ALU.mult)
                        t2, bt2 = qr_ring.next()
                        self.OP("dve", "tensor_tensor", [pxb, b_cs], [bt2], out=t2[:, :], in0=px[0:64, :], in1=cs_own[:, 1, cols], op=ALU.mult)
                        self.OP("dve", "tensor_tensor", [bt1, bt2], [b_qRT], out=qRT[:, h, cols], in0=t1[:, :], in1=t2[:, :], op=ALU.add)
            S.barrier()
            if self.debug:
                self.dbg_store("qNT", qNT, [128, 8, OWN], BF16, [b_qNT])
                self.dbg_store("qRT", qRT, [64, 8, OWN], BF16, [b_qRT])

            S.scope = "PB2load"
            with ExitStack() as sp2:
                cT = self.sb(sp2, "cT", [128, 2, SEQ], BF16)
                kpeR = self.sb(sp2, "kpeR", [64, SEQ], BF16)
                sqpe = self.sb(sp2, "sqpe", [64, SEQ], BF16)
                b_cT, b_kpeR, b_sqpe = S.buf("cTs", dma=True), S.buf("kpeRs", dma=True), S.buf("sqpes", dma=True)
                for g in range(2):
                    self.DMA(cT[:, g, :], self.s_cT[g, :, :], [self.B_cT], [b_cT])
                self.DMA(kpeR[:, :], self.s_kpeR[:, :], [self.B_kpeR], [b_kpeR])
                self.DMA(sqpe[:, :], self.s_sqpe[:, :], [self.B_sqpe], [b_sqpe])
                wukv = self.sb(sp2, "wukv", [128, 2, 2048], BF16)
                b_wukv = S.buf("wukv")
                with ExitStack() as wls:
                    wl = self.wloader(wls, "pb2")
                    for i in range(8):
                        wl.load(wukv, b_wukv, self.w_ukv[:, 256 * i:256 * (i + 1)], 2, 256, None, 256 * i)
                S.barrier()
                cmB = self.sb(sp2, "cmB", [128, 8, 128], BF16)
                b_cmB = S.buf("cmB")
                self.OP("dve", "tensor_scalar", [b_cst], [b_cmB], out=cmB[:, :, :],
                        in0=cst[:, C_CM:C_CM + 1024].rearrange("p (r s) -> p r s", r=8), scalar1=NEG, scalar2=None, op0=ALU.mult)
                KnT = self.sb(sp2, "KnT", [128, SEQ], BF16)
                KrT = self.sb(sp2, "KrT", [64, SEQ], BF16)
                Vh = self.sb(sp2, "Vh", [128, NB, 128], BF16)
                b_KnT = [S.buf(f"KnT{g}") for g in range(SEQ // 512)]
                b_KrT = [S.buf(f"KrT{g}") for g in range(SEQ // 512)]
                b_Vh = [S.buf(f"Vh{g}") for g in range(SEQ // 512)]
                sqs_ring = Ring(S, nc, sp2, "pb2sqs", [128, 512], BF16, 3)
                tmp_ring = Ring(S, nc, sp2, "pb2tmp", [128, 512], F32, 2)
                rr_ring = Ring(S, nc, sp2, "pb2rr", [128, 512], F32, 2)
                PT_ring = Ring(S, nc, sp2, "pb2PT", [128, 512], BF16, 4)
                rz_ring = Ring(S, nc, sp2, "pb2rz", [128, 512], F32, 2)
                uo_ring = Ring(S, nc, sp2, "pb2uo", [128, 512], BF16, 2)
                sz_ring = Ring(S, nc, sp2, "pb2sz", [128, 512], BF16, 2, dma=True)
                prw = self.psring([0, 1])
                psO = [(self.psb[4], self.pbuf[4]), (self.psb[5], self.pbuf[5])]
                psZ = [(self.psb[6], self.pbuf[6]), (self.psb[7], self.pbuf[7])]
                ngrp = self.dbg_groups or SEQ // 512
                nkb = ngrp * 4
                for h in range(nheads):
                    def build_group(g, h=h):
                        S.scope = f"PBbuild{h}"
                        cols = slice(g * 512, (g + 1) * 512)
                        psK, pbK = self.psb[2], self.pbuf[2]
                        for i in range(2):
                            self.MM(psK[:, :], wukv[:, i, h * 256:h * 256 + 128], cT[:, i, cols], i == 0, i == 1, [b_wukv, b_cT], [pbK])
                        s16, bs = sqs_ring.next()
                        self.OP("act", "activation", [pbK], [bs], out=s16[:, :], in_=psK[:, :], func=AF.Square)
                        pq, pqb = self.psb[3], self.pbuf[3]
                        self.MM(pq[:, :], ones[:, :], s16[:, :], True, False, [bs, b_const], [pqb])
                        self.MM(pq[:, :], ones[0:64, :], sqpe[:, cols], False, True, [b_sqpe, b_const], [pqb])
                        rr, brr = rr_ring.next()
                        self.rsqrt_ps(pq[:, :], pqb, 128, 512, 1.0 / 192, rr[:, :], brr, tmp_ring)
                        self.OP("dve", "tensor_tensor", [pbK, brr], [b_KnT[g]], out=KnT[:, cols], in0=psK[:, :], in1=rr[:, :], op=ALU.mult)
                        self.OP("pool", "tensor_tensor", [b_kpeR, brr], [b_KrT[g]], out=KrT[:, cols], in0=kpeR[:, cols], in1=rr[0:64, :],
                                op=ALU.mult)
                        psV, pbV = self.psb[3], self.pbuf[3]
                        for tb in range(4):
                            for i in range(2):
                                self.MM(psV[:, tb * 128:(tb + 1) * 128], cT[:, i, g * 512 + tb * 128:g * 512 + (tb + 1) * 128],
                                        wukv[:, i, h * 256 + 128:h * 256 + 256], i == 0, i == 1, [b_wukv, b_cT], [pbV])
                        self.OP("act", "activation", [pbV], [b_Vh[g]], out=Vh[:, g * 4:(g + 1) * 4, :],
                                in_=psV[:, :].rearrange("p (b d) -> p b d", b=4), func=AF.Copy)
                        S.scope = f"PBatt{h}"
                    S.scope = f"PBatt{h}"
                    built = [0]

                    def ensure_built(gidx):
                        gidx = min(gidx, ngrp - 1)
                        while built[0] <= gidx:
                            build_group(built[0])
                            built[0] += 1
                    units = []
                    for kb in range(nkb):
                        c_lo = (kb // 8) * 128
                        pieces = [(c_lo, 512), (512, OWN)] if c_lo < 512 else [(c_lo, OWN)]
                        for (c0, c1) in pieces:
                            units.append((kb, c0, c1, c0 == c_lo))
                    st = {}

                    def qk(u):
                        kb, c0, c1, diag = units[u]
                        ks = slice(kb * 128, (kb + 1) * 128)
                        n = c1 - c0
                        ps, pb = prw.next()
                        self.MM(ps[:, 0:n], KnT[:, ks], qNT[:, h, c0:c1], True, False, [b_KnT[kb // 4], b_qNT], [pb])
                        self.MM(ps[:, 0:n], KrT[:, ks], qRT[:, h, c0:c1], False, not diag, [b_KrT[kb // 4], b_qRT], [pb])
                        if diag:
                            self.MM(ps[:, 0:128], cmB[:, kb % 8, :], ident[:, :], False, True, [b_cmB, b_const], [pb],
                                    skip_group_check=True)
                        st[u] = (ps, pb)

                    def ex_pv(u):
                        kb, c0, c1, diag = units[u]
                        n = c1 - c0
                        bank = c0 // 512
                        o0 = c0 - bank * 512
                        ps, pb = st.pop(u)
                        PT, bPT = PT_ring.next()
                        self.OP("act", "activation", [pb], [bPT], out=PT[:, 0:n], in_=ps[:, 0:n], func=AF.Exp)
                        self.MM(psO[bank][0][:, o0:o0 + n], Vh[:, kb, :], PT[:, 0:n], kb == 0, kb == nkb - 1,
                                [b_Vh[kb // 4], bPT], [psO[bank][1]], skip_group_check=True)
                        self.MM(psZ[bank][0][:, o0:o0 + n], ones[:, :], PT[:, 0:n], kb == 0, kb == nkb - 1,
                                [b_const, bPT], [psZ[bank][1]], skip_group_check=True)
                    LOOK = 1
                    for u in range(min(LOOK, len(units))):
                        ensure_built(units[u][0] // 4 + 1)
                        qk(u)
                    for u in range(len(units)):
                        if u + LOOK < len(units):
                            ensure_built(units[u + LOOK][0] // 4 + 1)
                            qk(u + LOOK)
                        ex_pv(u)
                    ensure_built(ngrp - 1)
                    for tg in range(2):
                        cols = slice(tg * 512, (tg + 1) * 512)
                        sz, bsz = sz_ring.next()
                        self.DMA(sz[:, :], self.s_silu[1, h, :, cols], [self.B_silu[1]], [bsz])
                        rz, brz = rz_ring.next()
                        self.OP("dve", "reciprocal", [psZ[tg][1]], [brz], out=rz[:, :], in_=psZ[tg][0][:, :])
                        rz2, brz2 = rz_ring.next()
                        self.OP("dve", "tensor_tensor", [psO[tg][1], brz], [brz2], out=rz2[:, :], in0=psO[tg][0][:, :], in1=rz[:, :],
                                op=ALU.mult)
                        uo, buo = uo_ring.next()
                        self.OP("pool", "tensor_tensor", [brz2, bsz], [buo], out=uo[:, :], in0=rz2[:, :], in1=sz[:, :], op=ALU.mult)
                        self.DMA(self.s_u[1, h, :, cols], uo[:, :], [buo], [self.B_u[1]])

    def phaseC(self):
        S, nc = self.S, self.nc
        cst, b_cst, ones, b_const = self.cst, self.b_cst, self.ones, self.b_const
        w_in = self.w_in
        with ExitStack() as pc:
            ckT = self.sb(pc, "ckT", [128, 4, 2, 256], BF16)
            cv = self.sb(pc, "cv", [128, 2, 1024], BF16)
            cqT = self.sb(pc, "cqTc", [128, 8, OWN], BF16)
            b_ckT, b_cv, b_cqT = S.buf("ckT"), S.buf("cv"), S.buf("cqTc")
            gs = self.sb(pc, "gscC", [128, 2])
            b_gs = S.buf("gscC")
            self.OP("dve", "tensor_scalar", [b_cst], [b_gs], out=gs[:, 0:2], in0=cst[:, G_CQ:G_CQ + 2], scalar1=C_SCALE, scalar2=None,
                    op0=ALU.mult)
            sqs_ring = Ring(S, nc, pc, "pcsqs", [128, 512], BF16, 3)
            tmp_ring = Ring(S, nc, pc, "pctmp", [128, 512], F32, 2)
            rr_ring = Ring(S, nc, pc, "pcrr", [128, 512], F32, 2)
            with ExitStack() as sp0:
                wl = self.wloader(sp0, "pc0")
                wring = Ring(S, nc, sp0, "pc0wr", [128, KC, 256], BF16, 2)
                ms = self.sb(sp0, "pcms", [128, KC, 256])
                b_ms = S.buf("pcms", dma=True)
                self.DMA(ms[:, :, :], self.memT.rearrange("(k p) m -> p k m", p=128), [], [b_ms])
                msq = self.sb(sp0, "pcmsq", [128, KC, 256], BF16)
                hm = self.sb(sp0, "pchm", [128, KC, 256], BF16)
                rm = self.sb(sp0, "pcrm", [128, 256])
                b_msq, b_hm, b_rm = S.buf("pcmsq"), S.buf("pchm"), S.buf("pcrm")
                self.OP("act", "activation", [b_ms], [b_msq], out=msq[:, :, :], in_=ms[:, :, :], func=AF.Square)
                ps, pb = self.psb[0], self.pbuf[0]
                for k in range(KC):
                    self.MM(ps[:, 0:256], ones[:, :], msq[:, k, :], k == 0, k == KC - 1, [b_msq, b_const], [pb])
                self.rsqrt_ps(ps[:, 0:256], pb, 128, 256, 1.0 / D, rm[:, :], b_rm, tmp_ring)
                self.OP("dve", "tensor_tensor", [b_ms, b_rm], [b_hm], out=hm[:, :, :], in0=ms[:, :, :],
                        in1=rm[:, :].unsqueeze(1).to_broadcast([128, KC, 256]), op=ALU.mult)
                pr = self.psring([1, 2, 3, 4, 5, 6])

                def load(i):
                    wt, wb = wring.next()
                    wl.load(wt, wb, self.w_mem[:, 256 * i:256 * (i + 1)], KC, 256, G_MEM)
                    return wt, wb
                nxt = load(0)
                for i in range(8):
                    wt, wb = nxt
                    if i + 1 < 8:
                        nxt = load(i + 1)
                    if i < 4:
                        pss = []
                        for dc in range(2):
                            p2, pb2 = pr.next()
                            for k in range(KC):
                                self.MM(p2[:, 0:256], wt[:, k, dc * 128:(dc + 1) * 128], hm[:, k, :], k == 0, k == KC - 1, [wb, b_hm], [pb2])
                            pss.append((p2, pb2))
                        pz, pzb = pr.next()
                        for dc, (p2, pb2) in enumerate(pss):
                            s16, bs = sqs_ring.next()
                            self.OP("act", "activation", [pb2], [bs], out=s16[:, 0:256], in_=p2[:, 0:256], func=AF.Square)
                            self.MM(pz[:, 0:256], ones[:, :], s16[:, 0:256], dc == 0, dc == 1, [bs, b_const], [pzb])
                        rr, brr = rr_ring.next()
                        self.rsqrt_ps(pz[:, 0:256], pzb, 128, 256, 1.0 / 256, rr[:, 0:256], brr, tmp_ring)
                        for dc, (p2, pb2) in enumerate(pss):
                            self.OP("dve", "scalar_tensor_tensor", [pb2, brr, b_cst], [b_ckT], out=ckT[:, i, dc, :], in0=p2[:, 0:256],
                                    scalar=self.gcol(G_CK + dc), in1=rr[:, 0:256], op0=ALU.mult, op1=ALU.mult)
                    else:
                        for mb in range(2):
                            p2, pb2 = pr.next()
                            for k in range(KC):
                                self.MM(p2[:, 0:256], hm[:, k, mb * 128:(mb + 1) * 128], wt[:, k, :], k == 0, k == KC - 1, [wb, b_hm], [pb2])
                            self.OP("act", "activation", [pb2], [b_cv], out=cv[:, mb, (i - 4) * 256:(i - 3) * 256], in_=p2[:, 0:256],
                                    func=AF.Copy)
            S.barrier()
            with ExitStack() as sp1:
                hT, b_hT = self.load_hT_own(sp1)
                wl = self.wloader(sp1, "pc1")
                wring = Ring(S, nc, sp1, "pc1wr", [128, KC, 256], BF16, 2)
                so_ring = Ring(S, nc, sp1, "pcso", [128, 512], BF16, 2)
                pr = self.psring([0, 1, 2, 3, 4, 5])
                prz = self.psring([6, 7])
                held = {}

                def post(tag, idx, tg, ps, pb):
                    cols = slice(tg * 512, (tg + 1) * 512)
                    if tag == "cz":
                        so, bso = so_ring.next()
                        self.OP("act", "activation", [pb], [bso], out=so[:, :], in_=ps[:, :], func=AF.Silu)
                        self.DMA(self.s_silu[2, idx, :, cols], so[:, :], [bso], [self.B_silu[2]])
                        return
                    held[(idx % 2, tg)] = (ps, pb)
                    if idx % 2 == 1 and tg == 1:
                        for t2 in range(2):
                            c2 = slice(t2 * 512, (t2 + 1) * 512)
                            pz, pzb = prz.next()
                            for dc in range(2):
                                p2, pb2 = held[(dc, t2)]
                                s16, bs = sqs_ring.next()
                                self.OP("act", "activation", [pb2], [bs], out=s16[:, :], in_=p2[:, :], func=AF.Square)
                                self.MM(pz[:, :], ones[:, :], s16[:, :], dc == 0, dc == 1, [bs, b_const], [pzb])
                            rr, brr = rr_ring.next()
                            self.rsqrt_ps(pz[:, :], pzb, 128, 512, 1.0 / 256, rr[:, :], brr, tmp_ring)
                            for dc in range(2):
                                p2, pb2 = held[(dc, t2)]
                                self.OP("dve", "scalar_tensor_tensor", [pb2, brr, b_gs], [b_cqT], out=cqT[:, idx - 1 + dc, c2],
                                        in0=p2[:, :], scalar=gs[:, dc:dc + 1], in1=rr[:, :], op0=ALU.mult, op1=ALU.mult)

                specs = []
                for i in range(4):
                    specs.append((w_in[:, O_CQ + 256 * i:O_CQ + 256 * (i + 1)], 256, G_NORM,
                                  [(0, 128, "cq", 2 * i), (128, 128, "cq", 2 * i + 1)]))
                for i in range(4):
                    specs.append((w_in[:, O_CZ + 256 * i:O_CZ + 256 * (i + 1)], 256, G_NORM,
                                  [(0, 128, "cz", 2 * i), (128, 128, "cz", 2 * i + 1)]))
                self.own_proj(wl, wring, hT, b_hT, pr, specs, post)
            S.barrier()
            with ExitStack() as sp2:
                PT_ring = Ring(S, nc, sp2, "pcPT", [128, 512], BF16, 3)
                rz_ring = Ring(S, nc, sp2, "pcrz", [128, 512], F32, 2)
                uo_ring = Ring(S, nc, sp2, "pcuo", [128, 512], BF16, 2)
                sz_ring = Ring(S, nc, sp2, "pcsz", [128, 512], BF16, 2, dma=True)
                prw = self.psring([0, 1, 2])
                for h in range(4):
                    for tg in range(2):
                        cols = slice(tg * 512, (tg + 1) * 512)
                        psO = [(self.psb[4], self.pbuf[4]), (self.psb[5], self.pbuf[5])]
                        psZ = (self.psb[6], self.pbuf[6])
                        for mb in range(2):
                            ps, pb = prw.next()
                            for dc in range(2):
                                self.MM(ps[:, :], ckT[:, h, dc, mb * 128:(mb + 1) * 128], cqT[:, 2 * h + dc, cols], dc == 0, dc == 1,
                                        [b_ckT, b_cqT], [pb])
                            PT, bPT = PT_ring.next()
                            self.OP("act", "activation", [pb], [bPT], out=PT[:, :], in_=ps[:, :], func=AF.Exp)
                            for dvc in range(2):
                                self.MM(psO[dvc][0][:, :], cv[:, mb, h * 256 + dvc * 128:h * 256 + (dvc + 1) * 128], PT[:, :], mb == 0, mb == 1,
                                        [b_cv, bPT], [psO[dvc][1]])
                            self.MM(psZ[0][:, :], ones[:, :], PT[:, :], mb == 0, mb == 1, [b_const, bPT], [psZ[1]])
                        rz, brz = rz_ring.next()
                        self.OP("dve", "reciprocal", [psZ[1]], [brz], out=rz[:, :], in_=psZ[0][:, :])
                        for dvc in range(2):
                            sz, bsz = sz_ring.next()
                            self.DMA(sz[:, :], self.s_silu[2, 2 * h + dvc, :, cols], [self.B_silu[2]], [bsz])
                            rz2, brz2 = rz_ring.next()
                            self.OP("dve", "tensor_tensor", [psO[dvc][1], brz], [brz2], out=rz2[:, :], in0=psO[dvc][0][:, :], in1=rz[:, :],
                                    op=ALU.mult)
                            uo, buo = uo_ring.next()
                            self.OP("pool", "tensor_tensor", [brz2, bsz], [buo], out=uo[:, :], in0=rz2[:, :], in1=sz[:, :], op=ALU.mult)
                            self.DMA(self.s_u[2, 2 * h + dvc, :, cols], uo[:, :], [buo], [self.B_u[2]])

    def phaseF(self):
        S, nc = self.S, self.nc
        cst, b_cst = self.cst, self.b_cst
        w_in = self.w_in
        with ExitStack() as pf:
            mT = self.sb(pf, "mergedT", [128, KC, OWN], BF16)
            b_mT = [S.buf(f"mergedT{i}") for i in range(KC)]
            wl = self.wloader(pf, "pf")
            wring = Ring(S, nc, pf, "pfwr", [128, KC, 256], BF16, 4)
            with ExitStack() as sp1:
                hT, b_hT = self.load_hT_own(sp1)
                uT = self.sb(sp1, "uT", [128, 24, OWN], BF16)
                b_uT = S.buf("uT", dma=True)
                for n in range(3):
                    self.DMA(uT[:, n * 8:(n + 1) * 8, :], self.s_u[n].rearrange("c p t -> p c t"), [self.B_u[n]], [b_uT])
                sg_ring = Ring(S, nc, sp1, "pfsg", [128, 512], F32, 2)
                mac_ring = Ring(S, nc, sp1, "pfmac", [128, 512], F32, 4)
                macs = {}
                tm_ring = Ring(S, nc, sp1, "pftm", [128, 512], F32, 2)
                pr = self.psring([0, 1, 2, 3, 4, 5, 6, 7])
                jobs = []
                for ds in range(8):
                    for n in range(3):
                        jobs.append(("br", ds, n))
                        jobs.append(("gt", ds, n))

                def load(job):
                    kind, ds, n = job
                    wt, wb = wring.next()
                    if kind == "br":
                        wl.load(wt, wb, self.w_br[n, :, ds * 256:(ds + 1) * 256], 8, 256, None)
                    else:
                        wl.load(wt, wb, w_in[:, O_G + n * D + ds * 256:O_G + n * D + (ds + 1) * 256], KC, 256, G_NORM)
                    return wt, wb
                loaded = [load(jobs[0]), load(jobs[1])]
                for ji in range(0, len(jobs), 2):
                    _, ds, n = jobs[ji]
                    (wbr, bwbr), (wgt, bwgt) = loaded
                    loaded = [load(jobs[ji + 2]), load(jobs[ji + 3])] if ji + 2 < len(jobs) else None
                    for dcl in range(2):
                        dc = ds * 2 + dcl
                        for tg in range(2):
                            cols = slice(tg * 512, (tg + 1) * 512)
                            pg, pgb = pr.next()
                            for k in range(KC):
                                self.MM(pg[:, :], wgt[:, k, dcl * 128:(dcl + 1) * 128], hT[:, k, cols], k == 0, k == KC - 1, [bwgt, b_hT], [pgb])
                            sg, bsg = sg_ring.next()
                            self.OP("act", "activation", [pgb], [bsg], out=sg[:, :], in_=pg[:, :], func=AF.Sigmoid)
                            py, pyb = pr.next()
                            for k in range(8):
                                self.MM(py[:, :], wbr[:, k, dcl * 128:(dcl + 1) * 128], uT[:, n * 8 + k, cols], k == 0, k == 7, [bwbr, b_uT], [pyb])
                            key = (dc, tg)
                            if n == 0:
                                mac, bmac = mac_ring.next()
                                macs[key] = (mac, bmac)
                                self.OP("dve", "tensor_tensor", [pyb, bsg], [bmac], out=mac[:, :], in0=py[:, :], in1=sg[:, :], op=ALU.mult)
                            else:
                                mac, bmac = macs[key]
                                tm, btm = tm_ring.next()
                                self.OP("dve", "tensor_tensor", [pyb, bsg], [btm], out=tm[:, :], in0=py[:, :], in1=sg[:, :], op=ALU.mult)
                                if n == 1:
                                    self.OP("pool", "tensor_tensor", [btm, bmac], [bmac], out=mac[:, :], in0=mac[:, :], in1=tm[:, :], op=ALU.add)
                                else:
                                    self.OP("pool", "tensor_tensor", [btm, bmac], [b_mT[dc]], out=mT[:, dc, cols], in0=mac[:, :], in1=tm[:, :],
                                            op=ALU.add)
            S.barrier()
            S.scope = "PF2"
            with ExitStack() as sp2:
                xo_ring = Ring(S, nc, sp2, "pfxo", [128, OWN], F32, 2, dma=True)
                oo_ring = Ring(S, nc, sp2, "pfoo", [128, 512], F32, 3)
                pr = self.psring([0, 1, 2, 3, 4, 5, 6, 7])
                xsrc = self.xT_own.rearrange("(k p) t -> p k t", p=128)
                osrc = self.outT.rearrange("(k p) t -> p k t", p=128)

                def load(i):
                    wt, wb = wring.next()
                    wl.load(wt, wb, self.w_out[:, i * 256:(i + 1) * 256], KC, 256, None)
                    return wt, wb
                nxt = load(0)
                for i in range(8):
                    wt, wb = nxt
                    if i + 1 < 8:
                        nxt = load(i + 1)
                    for ocl in range(2):
                        oc = 2 * i + ocl
                        xo, bxo = xo_ring.next()
                        self.DMA(xo[:, :], xsrc[:, oc, :], [], [bxo])
                        for tg in range(2):
                            cols = slice(tg * 512, (tg + 1) * 512)
                            ps, pb = pr.next()
                            for k in range(KC):
                                self.MM(ps[:, :], wt[:, k, ocl * 128:(ocl + 1) * 128], mT[:, k, cols], k == 0, k == KC - 1, [wb, b_mT[k]], [pb])
                            oo, boo = oo_ring.next()
                            self.OP("dve", "tensor_tensor", [pb, bxo], [boo], out=oo[:, :], in0=ps[:, :], in1=xo[:, cols], op=ALU.add)
                            self.DMA(osrc[:, oc, cols], oo[:, :], [boo], [self.B_out])

def _t5_bucket_np(n):
    n = np.maximum(n, 0)
    nf = np.maximum(n, 1).astype(np.float32)
    large = 16 + (np.log(nf / np.float32(16)) / np.float32(math.log(128 / 16)) * np.float32(16)).astype(np.int32)
    large = np.minimum(large, 31)
    return np.where(n < 16, n, large)


def _const_tile(core, gains):
    c = np.zeros((128, NCST), np.float32)
    c[:, C_ID:C_ID + 128] = np.eye(128, dtype=np.float32)
    t = np.arange(128)[:, None, None]
    r = np.arange(8)[None, :, None]
    s = np.arange(128)[None, None, :]
    masked = (r > core) | ((r == core) & (s > t))
    c[:, C_CM:C_CM + 1024] = masked.astype(np.float32).reshape(128, 1024)
    ss = np.arange(128)[:, None, None]
    dl = np.arange(2)[None, :, None]
    tt = np.arange(128)[None, None, :]
    c[:, C_BK:C_BK + 256] = _t5_bucket_np(128 * dl + tt - ss).astype(np.float32).reshape(128, 256)
    c[:, C_G:C_G + gains.shape[1]] = gains
    invf = 1.0 / (np.float32(10000.0) ** (np.arange(0, 64, 2, dtype=np.float32) / np.float32(64)))
    c[0:64, C_INVF] = np.concatenate([invf, invf]).astype(np.float32)
    pm = np.zeros((64, 64), np.float32)
    for m in range(32):
        pm[m + 32, m] = -1.0
        pm[m, m + 32] = 1.0
    c[0:64, C_PM:C_PM + 64] = pm
    sel = np.zeros(18, np.float32)
    for rr in range(9):
        dlt = core + 1 - rr
        if dlt == 0:
            sel[rr] = 1.0
        if dlt == 1:
            sel[9 + rr] = 1.0
    c[:, C_SEL:C_SEL + 18] = sel[None, :]
    c[:, C_PW:C_PW + NITER] = (0.5 ** np.arange(1, NITER + 1, dtype=np.float32))[None, :]
    return c


def _pack_gains(norm_g, mem_norm_g, a_q, a_k, b_qlat, b_kvlat, b_q, b_k, c_q, c_k):
    g = np.zeros((128, C_INVF - C_G), np.float32)

    def put(col, vec, n):
        v = np.asarray(vec, np.float32).reshape(-1)
        nch = (v.size + 127) // 128
        for i in range(nch):
            seg = v[i * 128:(i + 1) * 128]
            g[:seg.size, col - C_G + i] = seg
    put(G_NORM, norm_g, 16)
    put(G_MEM, mem_norm_g, 16)
    put(G_AQ, a_q, 1)
    put(G_AK, a_k, 1)
    put(G_QLAT, b_qlat, 4)
    put(G_KVLAT, b_kvlat, 2)
    bq = np.asarray(b_q, np.float32).reshape(-1)
    bk = np.asarray(b_k, np.float32).reshape(-1)
    put(G_BQN, bq[:128], 1)
    put(G_BQR, bq[128:], 1)
    put(G_BKN, bk[:128], 1)
    put(G_BKR, bk[128:], 1)
    put(G_CQ, c_q, 2)
    put(G_CK, c_k, 2)
    return g


_PROG_CACHE = {}


def make_in_maps(x, mem, positions, rel_bias, norm_g, mem_norm_g, w_in, a_q_norm_g, a_k_norm_g, b_q_lat_norm_g,
                 b_kv_lat_norm_g, w_b_uq, w_b_ukv, b_q_norm_g, b_k_norm_g, w_mem_kv, c_q_norm_g, c_k_norm_g,
                 w_branch, w_out):
    x2 = np.asarray(x, np.float32)[0]
    xT = np.ascontiguousarray(x2.T)
    pos = np.asarray(positions, np.int32)[0]
    gains = _pack_gains(norm_g[0], mem_norm_g[0], a_q_norm_g[0], a_k_norm_g[0], b_q_lat_norm_g[0], b_kv_lat_norm_g[0],
                        b_q_norm_g[0], b_k_norm_g[0], c_q_norm_g[0], c_k_norm_g[0])
    shared = {
        "xT_all": xT,
        "pos_all": np.ascontiguousarray(pos[None, :]),
        "memT": np.ascontiguousarray(np.asarray(mem, np.float32)[0].T),
        "w_in": np.ascontiguousarray(np.asarray(w_in, np.float32)[0]),
        "w_uq": np.ascontiguousarray(np.asarray(w_b_uq, np.float32)[0]),
        "w_ukv": np.ascontiguousarray(np.asarray(w_b_ukv, np.float32)[0]),
        "w_mem": np.ascontiguousarray(np.asarray(w_mem_kv, np.float32)[0]),
        "w_br": np.ascontiguousarray(np.asarray(w_branch, np.float32)[0]),
        "w_out": np.ascontiguousarray(np.asarray(w_out, np.float32)[0]),
        "relb": np.ascontiguousarray(np.asarray(rel_bias, np.float32).reshape(1, 256)),
    }
    in_maps = []
    own_idx = []
    for c in range(NCORES):
        idx = np.concatenate([np.arange((c + 8 * j) * 128, (c + 8 * j + 1) * 128) for j in range(NSLOT)])
        own_idx.append(idx)
        m = dict(shared)
        m["xT_own"] = np.ascontiguousarray(xT[:, idx])
        m["pos_own"] = np.ascontiguousarray(pos[idx][None, :])
        m["cst"] = _const_tile(c, gains)
        in_maps.append(m)
    return in_maps, own_idx


def kernel(**inputs):
    in_maps, own_idx = make_in_maps(**inputs)
    if "prog" not in _PROG_CACHE:
        _PROG_CACHE["prog"] = Prog().build()
    nc = _PROG_CACHE["prog"]
    res = run_bass_kernel_spmd(nc, in_maps, core_ids=list(range(NCORES)))
    out = np.zeros((1, SEQ, D), np.float32)
    for c in range(NCORES):
        out[0, own_idx[c], :] = np.asarray(res.results[c]["outT"], np.float32).T
    return out
```

```python
import math
from contextlib import ExitStack

import numpy as np
import concourse.bass as bass
import concourse.mybir as mybir
from concourse.bass_utils import run_bass_kernel_spmd

F32 = mybir.dt.float32
BF16 = mybir.dt.bfloat16
I32 = mybir.dt.int32
ALU = mybir.AluOpType
AF = mybir.ActivationFunctionType
AX = mybir.AxisListType

NCORES = 8
SEQ = 8192
D = 2048
KC = 16
OWN = SEQ // NCORES
NSLOT = OWN // 128
NB = SEQ // 128
EPS = 1e-6
NITER = 14
TOPK = 256
NEG = -30000.0

O_AQ, O_AK, O_AV, O_AZ, O_IQ, O_IK, O_IW = 0, 1024, 1280, 1536, 2560, 3584, 3648
O_BCQ, O_BCKV, O_BKPE, O_BZ, O_CQ, O_CZ, O_G = 3664, 4176, 4432, 4496, 5520, 6544, 7568
IN_W = 13712

A_SCALE = 128 ** -0.5
B_SCALE = 192 ** -0.5
C_SCALE = 256 ** -0.5
TWO_PI = 2.0 * math.pi
CW1 = 6.28125
CW2 = TWO_PI - CW1

C_ID = 0
C_CM = C_ID + 128
C_BK = C_CM + 1024
C_G = C_BK + 256
G_NORM = C_G
G_MEM = G_NORM + 16
G_AQ = G_MEM + 16
G_AK = G_AQ + 1
G_QLAT = G_AK + 1
G_KVLAT = G_QLAT + 4
G_BQN = G_KVLAT + 2
G_BQR = G_BQN + 1
G_BKN = G_BQR + 1
G_BKR = G_BKN + 1
G_CQ = G_BKR + 1
G_CK = G_CQ + 2
C_INVF = G_CK + 2
C_PM = C_INVF + 1
C_SEL = C_PM + 64
C_PW = C_SEL + 18
NCST = C_PW + NITER


class Buf:
    __slots__ = ("name", "last_w", "readers", "dsem", "dcount")

    def __init__(self, name):
        self.name = name
        self.last_w = None
        self.readers = []
        self.dsem = None
        self.dcount = 0


class Sched:
    def __init__(self, nc, stack):
        self.nc = nc
        self.stack = stack
        self.sems = {}
        self.streams = {k: [] for k in ("pe", "act", "dve", "pool", "sp")}
        self.count = {k: 0 for k in self.streams}
        self.known = {k: {} for k in self.streams}
        for k in self.streams:
            self._sem("eng_" + k)
        self.store_count = 0
        self._sem("store")
        self.dbufs = []
        self.n_ins = 0
        self.scope = None
        self.use_scopes = False

    def _sem(self, key):
        if key not in self.sems:
            self.sems[key] = self.stack.enter_context(self.nc.semaphore(key))
        return key

    def buf(self, name, dma=False):
        b = Buf(name)
        if dma:
            b.dsem = self._sem("d_" + name)
            self.dbufs.append(b)
        return b

    def _deps(self, eng, reads, writes):
        deps = {}
        own = "eng_" + eng

        def add(ev, raw):
            if ev is None:
                return
            k, v = ev
            if k == "store":
                v = self.store_count
            if k == own:
                if eng in ("pe", "sp"):
                    return
                if not raw:
                    return
            if deps.get(k, 0) < v:
                deps[k] = v
        for b in reads:
            add(b.last_w, True)
        for b in writes:
            add(b.last_w, False)
            for r in b.readers:
                add(r, False)
        out = []
        kn = self.known[eng]
        for k, v in deps.items():
            if kn.get(k, 0) >= v:
                continue
            kn[k] = v
            out.append((k, v))
        return out

    def op(self, eng, fn, reads=(), writes=()):
        waits = self._deps(eng, reads, writes)
        self.count[eng] += 1
        ev = ("eng_" + eng, self.count[eng])
        self.streams[eng].append((waits, fn, [("eng_" + eng, 1)], self.scope))
        for b in reads:
            b.readers.append(ev)
        for b in writes:
            b.last_w = ev
            b.readers = []
        self.n_ins += 1
        return ev

    def dma(self, fn, reads=(), writes=(), q="sp"):
        waits = self._deps(q, reads, writes)
        dst = [b for b in writes if b.dsem is not None]
        if dst:
            b0 = dst[0]
            b0.dcount += 16
            ev = (b0.dsem, b0.dcount)
            incs = [(b0.dsem, 16)]
        else:
            self.store_count += 16
            ev = ("store", self.store_count)
            incs = [("store", 16)]
        self.streams[q].append((waits, fn, incs, self.scope))
        for b in reads:
            b.readers.append(ev)
        for b in writes:
            b.last_w = ev
            b.readers = []
        self.n_ins += 1
        return ev

    def barrier(self):
        for eng in self.streams:
            waits = []
            kn = self.known[eng]
            for o in self.streams:
                if o == eng:
                    continue
                k = "eng_" + o
                if self.count[o] > kn.get(k, 0):
                    kn[k] = self.count[o]
                    waits.append((k, self.count[o]))
            if self.store_count > kn.get("store", 0):
                kn["store"] = self.store_count
                waits.append(("store", self.store_count))
            for b in self.dbufs:
                if b.dcount > kn.get(b.dsem, 0):
                    kn[b.dsem] = b.dcount
                    waits.append((b.dsem, b.dcount))
            if waits:
                self.streams[eng].append((waits, None, [], self.scope))

    def emit(self):
        nc = self.nc
        sems = self.sems
        engmap = {"pe": "tensor", "act": "scalar", "dve": "vector", "pool": "gpsimd", "sp": "sync"}

        def make(key):
            stream = self.streams[key]

            def one(e, waits, fn, incs):
                for k, v in waits:
                    e.wait_ge(sems[k], v)
                if fn is None:
                    return
                if isinstance(fn, tuple):
                    ins = getattr(e, fn[0])(**fn[1])
                else:
                    ins = fn(e)
                for k, v in incs:
                    ins = ins.then_inc(sems[k], v)

            def body(e):
                if not self.use_scopes:
                    for waits, fn, incs, _ in stream:
                        one(e, waits, fn, incs)
                    return
                i = 0
                while i < len(stream):
                    sc = stream[i][3]
                    j = i
                    while j < len(stream) and stream[j][3] == sc:
                        j += 1
                    if sc is None:
                        for waits, fn, incs, _ in stream[i:j]:
                            one(e, waits, fn, incs)
                    else:
                        with nc.named_scope(sc):
                            for waits, fn, incs, _ in stream[i:j]:
                                one(e, waits, fn, incs)
                    i = j
            return body
        with nc.Block() as block:
            for key, attr in engmap.items():
                if self.streams[key]:
                    getattr(block, attr)(make(key))


_UID = [0]


def _uid():
    _UID[0] += 1
    return _UID[0]


class Ring:
    def __init__(self, S, nc, stack, name, shape, dt, n, dma=False, psum=False):
        self.items = []
        for i in range(n):
            if psum:
                t = stack.enter_context(nc.psum_tensor(f"rp{_uid()}_{name}{i}", shape, dt))
            else:
                t = stack.enter_context(nc.sbuf_tensor(f"r{_uid()}_{name}{i}", shape, dt))
            self.items.append((t, S.buf(f"{name}{i}_{_uid()}", dma=dma)))
        self.i = 0

    def next(self):
        it = self.items[self.i % len(self.items)]
        self.i += 1
        return it


class Prog:
    def __init__(self, debug=False, stop_after=None, dbg_groups=None, dbg_slots=None, dbg_skip=(), scopes=False):
        self.debug = debug
        self.stop_after = stop_after
        self.dbg_groups = dbg_groups
        self.dbg_slots = dbg_slots
        self.dbg_skip = dbg_skip
        self.scopes = scopes
        self.nc = nc = bass.Bass("TRN2", target_bir_lowering=False)
        dk = "ExternalOutput" if debug else "Internal"

        def din(name, shape, dt=F32):
            return nc.dram_tensor(name, shape, dt, kind="ExternalInput").ap()

        self.xT_all = din("xT_all", [D, SEQ])
        self.xT_own = din("xT_own", [D, OWN])
        self.pos_all = din("pos_all", [1, SEQ], I32)
        self.pos_own = din("pos_own", [1, OWN], I32)
        self.memT = din("memT", [D, 256])
        self.w_in = din("w_in", [D, IN_W])
        self.w_uq = din("w_uq", [512, 1536])
        self.w_ukv = din("w_ukv", [256, 2048])
        self.w_mem = din("w_mem", [D, 2048])
        self.w_br = din("w_br", [3, 1024, D])
        self.w_out = din("w_out", [D, D])
        self.relb = din("relb", [1, 256])
        self.cst_d = din("cst", [128, NCST])
        self.outT = nc.dram_tensor("outT", [D, OWN], F32, kind="ExternalOutput").ap()

        def scr(name, shape, dt=BF16):
            return nc.dram_tensor(name, shape, dt, kind=dk).ap()

        self.s_kTA = scr("s_kTA", [2, 128, SEQ])
        self.s_VA = scr("s_VA", [SEQ, 256])
        self.s_kiT = scr("s_kiT", [128, SEQ])
        self.s_cT = scr("s_cT", [2, 128, SEQ])
        self.s_kpeR = scr("s_kpeR", [64, SEQ])
        self.s_sqpe = scr("s_sqpe", [64, SEQ])
        self.s_hT = scr("s_hT", [128, KC, OWN])
        self.s_silu = scr("s_silu", [3, 8, 128, OWN])
        self.s_u = scr("s_u", [3, 8, 128, OWN])
        self.s_qTA = scr("s_qTA", [128, 8, OWN])
        self.s_qiT = scr("s_qiT", [128, 8, OWN])
        if debug:
            self.dbg = {}

    def sb(self, stack, name, shape, dt=F32):
        return stack.enter_context(self.nc.sbuf_tensor(f"t{_uid()}_{name}", shape, dt))

    def gcol(self, c, rows=128):
        return self.cst[0:rows, c:c + 1]

    def build(self):
        nc = self.nc
        top = ExitStack()
        with top:
            self.top = top
            S = self.S = Sched(nc, top)
            S.use_scopes = self.scopes
            for n in ("kTA", "VA", "kiT", "cT", "kpeR", "sqpe", "hT", "out", "qTA", "qiT"):
                setattr(self, "B_" + n, S.buf(n))
            self.B_silu = [S.buf(f"silu{i}") for i in range(3)]
            self.B_u = [S.buf(f"u{i}") for i in range(3)]

            cst = self.cst = self.sb(top, "cst", [128, NCST])
            b_cst = self.b_cst = S.buf("cst", dma=True)
            S.dma(lambda e: e.dma_start(out=cst[:], in_=self.cst_d[:, :]), writes=[b_cst])
            ident = self.ident = self.sb(top, "ident", [128, 128], BF16)
            ident4 = self.ident4 = self.sb(top, "ident4", [128, 512], BF16)
            ones = self.ones = self.sb(top, "ones", [128, 128], BF16)
            b_const = self.b_const = S.buf("const")
            S.op("dve", lambda e: e.tensor_copy(out=ident[:], in_=cst[:, C_ID:C_ID + 128]), reads=[b_cst], writes=[b_const])
            for i in range(4):
                S.op("pool", lambda e, i=i: e.tensor_copy(out=ident4[:, i * 128:(i + 1) * 128], in_=cst[:, C_ID:C_ID + 128]),
                     reads=[b_cst], writes=[b_const])
            S.op("pool", lambda e: e.memset(ones[:], 1.0), writes=[b_const])
            self.psb = [top.enter_context(nc.psum_tensor(f"ps{i}", [128, 512], F32)) for i in range(8)]
            self.pbuf = [S.buf(f"ps{i}") for i in range(8)]

            phases = [("P0", self.phase0), ("P1", self.phase1), ("PA", self.phaseA), ("PB", self.phaseB),
                      ("PC", self.phaseC), ("PF", self.phaseF)]
            for name, fn in phases:
                if name in self.dbg_skip:
                    continue
                S.scope = name
                fn()
                S.barrier()
                if self.stop_after == name:
                    break
            S.barrier()
            S.emit()
        return nc

    def psring(self, idxs):
        prog = self

        class PsRing:
            def __init__(self):
                self.i = 0

            def next(self):
                k = idxs[self.i % len(idxs)]
                self.i += 1
                return prog.psb[k], prog.pbuf[k]
        return PsRing()

    def rsqrt_ps(self, ps_ap, psbuf, n_part, width, inv_n, out_ap, out_buf, tmp_ring):
        S = self.S
        tmp, tb = tmp_ring.next()
        S.op("act", ("activation", dict(out=tmp[0:n_part, 0:width], in_=ps_ap, func=AF.Sqrt, bias=self.gcol_eps(n_part), scale=inv_n)),
             reads=[psbuf, self.b_const], writes=[tb])
        S.op("dve", ("reciprocal", dict(out=out_ap, in_=tmp[0:n_part, 0:width])), reads=[tb], writes=[out_buf])

    def gcol_eps(self, rows=128):
        return self.epsc[0:rows, 0:1]

    def trig_alloc(self, stack, ncols, tag):
        S = self.S
        t = {}
        t["pi"] = Ring(S, self.nc, stack, f"tgpi{tag}", [64, ncols], I32, 2, dma=True)
        for n, dt in (("a", F32), ("u", F32), ("k", I32), ("kf", F32)):
            t[n] = (self.sb(stack, f"tg{n}{tag}", [64, ncols], dt), S.buf(f"tg{n}{tag}"))
        return t

    def trig(self, t, pos_ap, cs_tile, cs_buf):
        S = self.S
        cst, b_cst = self.cst, self.b_cst
        pi_t, b_pi = t["pi"].next()
        (a_t, b_a), (u_t, b_u), (k_t, b_k), (kf_t, b_kf) = t["a"], t["u"], t["k"], t["kf"]
        S.dma(lambda e: e.dma_start(out=pi_t[:], in_=pos_ap.partition_broadcast(64)), writes=[b_pi])
        S.op("dve", lambda e: e.tensor_copy(out=a_t[:], in_=pi_t[:]), reads=[b_pi], writes=[b_a])
        S.op("dve", lambda e: e.tensor_scalar(out=a_t[:], in0=a_t[:], scalar1=cst[0:64, C_INVF:C_INVF + 1], scalar2=None,
                                              op0=ALU.mult), reads=[b_a, b_cst], writes=[b_a])
        for idx, shift in ((1, 0.0), (0, 0.5 * math.pi)):
            S.op("dve", lambda e, shift=shift: e.tensor_scalar(out=u_t[:], in0=a_t[:], scalar1=shift, scalar2=None, op0=ALU.add),
                 reads=[b_a], writes=[b_u])
            S.op("dve", lambda e: e.tensor_scalar(out=k_t[:], in0=u_t[:], scalar1=1.0 / TWO_PI, scalar2=None, op0=ALU.mult),
                 reads=[b_u], writes=[b_k])
            S.op("dve", lambda e: e.tensor_copy(out=kf_t[:], in_=k_t[:]), reads=[b_k], writes=[b_kf])
            S.op("dve", lambda e: e.scalar_tensor_tensor(out=u_t[:], in0=kf_t[:], scalar=-CW1, in1=u_t[:], op0=ALU.mult, op1=ALU.add),
                 reads=[b_kf, b_u], writes=[b_u])
            S.op("dve", lambda e: e.scalar_tensor_tensor(out=u_t[:], in0=kf_t[:], scalar=-CW2, in1=u_t[:], op0=ALU.mult, op1=ALU.add),
                 reads=[b_kf, b_u], writes=[b_u])
            S.op("dve", lambda e: e.tensor_scalar(out=u_t[:], in0=u_t[:], scalar1=math.pi, scalar2=-math.pi, op0=ALU.min, op1=ALU.max),
                 reads=[b_u], writes=[b_u])
            S.op("act", lambda e, idx=idx: e.activation(out=cs_tile[:, idx, :], in_=u_t[:], func=AF.Sin), reads=[b_u], writes=[cs_buf])

    def wloader(self, stack, tag, nstage=2, pool_share=True):
        prog = self
        S = self.S

        class WLoader:
            def __init__(self):
                self.stage = Ring(S, prog.nc, stack, f"wst{tag}", [128, KC, 256], F32, nstage, dma=True)
                self.flip = 0

            def load(self, dst_tile, dst_buf, dram_ap, nk, ncols, gain_col=None, dst_c0=0):
                st, sbf = self.stage.next()
                src = dram_ap.rearrange("(k p) c -> p k c", p=128)
                S.dma(lambda e: e.dma_start(out=st[:, 0:nk, 0:ncols], in_=src), writes=[sbf])
                if gain_col is None:
                    h1 = (2 * nk + 2) // 3 if pool_share else nk
                    S.op("dve", ("tensor_copy", dict(out=dst_tile[:, 0:h1, dst_c0:dst_c0 + ncols], in_=st[:, 0:h1, 0:ncols])),
                         reads=[sbf], writes=[dst_buf])
                    if h1 < nk:
                        S.op("pool", ("tensor_copy", dict(out=dst_tile[:, h1:nk, dst_c0:dst_c0 + ncols], in_=st[:, h1:nk, 0:ncols])),
                             reads=[sbf], writes=[dst_buf])
                else:
                    for k in range(nk):
                        eng = ("dve", "dve", "pool")[self.flip % 3] if pool_share else "dve"
                        self.flip += 1
                        if eng == "dve":
                            S.op(eng, ("tensor_scalar", dict(out=dst_tile[:, k, dst_c0:dst_c0 + ncols], in0=st[:, k, 0:ncols],
                                                             scalar1=prog.gcol(gain_col + k), scalar2=None, op0=ALU.mult)),
                                 reads=[sbf, prog.b_cst], writes=[dst_buf])
                        else:
                            S.op(eng, ("tensor_scalar", dict(out=dst_tile[:, k, dst_c0:dst_c0 + ncols], in0=st[:, k, 0:ncols],
                                                             scalar1=prog.gcol(gain_col + k), scalar2=0.0, op0=ALU.mult, op1=ALU.add)),
                                 reads=[sbf, prog.b_cst], writes=[dst_buf])
        return WLoader()

    def normalize_tokens(self, xs, bxs, sq, bsq, ps, pb, rstd_ap, brstd, hT, bhT, tmp_ring):
        S = self.S
        ones = self.ones
        S.op("act", lambda e: e.activation(out=sq[:], in_=xs[:], func=AF.Square), reads=[bxs], writes=[bsq])
        for k in range(KC):
            S.op("pe", lambda e, k=k: e.matmul(ps[:, :], lhsT=ones[:, :], rhs=sq[:, k, :], start=(k == 0), stop=(k == KC - 1)),
                 reads=[bsq, self.b_const], writes=[pb])
        self.rsqrt_ps(ps[:, :], pb, 128, 512, 1.0 / D, rstd_ap, brstd, tmp_ring)
        S.op("dve", lambda e: e.tensor_tensor(out=hT[:, 0:10, :], in0=xs[:, 0:10, :],
                                              in1=rstd_ap.unsqueeze(1).to_broadcast([128, 10, 512]), op=ALU.mult),
             reads=[bxs, brstd], writes=[bhT])
        S.op("pool", lambda e: e.tensor_tensor(out=hT[:, 10:16, :], in0=xs[:, 10:16, :],
                                               in1=rstd_ap.unsqueeze(1).to_broadcast([128, 6, 512]), op=ALU.mult),
             reads=[bxs, brstd], writes=[bhT])

    def phase0(self):
        S, nc = self.S, self.nc
        self.epsc = self.sb(self.top, "epsc", [128, 1])
        S.op("pool", lambda e: e.memset(self.epsc[:], EPS), writes=[self.b_const])
        with ExitStack() as ph:
            self.rstd_own = self.sb(ph, "rstd_own", [128, OWN])
            self.b_rstd_own = S.buf("rstd_own")
            xs_ring = Ring(S, nc, ph, "p0xs", [128, KC, 512], F32, 1, dma=True)
            sq = self.sb(ph, "p0sq", [128, KC, 512], BF16)
            bsq = S.buf("p0sq")
            hT = self.sb(ph, "p0hT", [128, KC, 512], BF16)
            bhT = S.buf("p0hT")
            tmp_ring = Ring(S, nc, ph, "p0tmp", [128, 512], F32, 2)
            xsrc = self.xT_own.rearrange("(k p) t -> p k t", p=128)
            for tg in range(2):
                cols = slice(tg * 512, (tg + 1) * 512)
                xs, bxs = xs_ring.next()
                S.dma(lambda e, xs=xs, cols=cols: e.dma_start(out=xs[:], in_=xsrc[:, :, cols]), writes=[bxs])
                self.normalize_tokens(xs, bxs, sq, bsq, self.psb[tg], self.pbuf[tg], self.rstd_own[:, cols], self.b_rstd_own,
                                      hT, bhT, tmp_ring)
                S.dma(lambda e, cols=cols: e.dma_start(out=self.s_hT[:, :, cols], in_=hT[:]), reads=[bhT], writes=[self.B_hT], q="act")

    def phase1(self):
        S, nc = self.S, self.nc
        cst, b_cst, ones, b_const = self.cst, self.b_cst, self.ones, self.b_const
        w_in = self.w_in
        with ExitStack() as ph:
            Wk = self.sb(ph, "Wk", [128, KC, 960], BF16)
            b_Wk = S.buf("Wk")
            with ExitStack() as wls:
                wl = self.wloader(wls, "p1")
                wl.load(Wk, b_Wk, w_in[:, O_AK:O_AK + 256], KC, 256, G_NORM, 0)
                wl.load(Wk, b_Wk, w_in[:, O_IK:O_IK + 64], KC, 64, G_NORM, 256)
                wl.load(Wk, b_Wk, w_in[:, O_IK:O_IK + 64], KC, 64, G_NORM, 320)
                wl.load(Wk, b_Wk, w_in[:, O_BCKV:O_BCKV + 256], KC, 256, G_NORM, 384)
                wl.load(Wk, b_Wk, w_in[:, O_BKPE:O_BKPE + 64], KC, 64, G_NORM, 640)
                wl.load(Wk, b_Wk, w_in[:, O_AV:O_AV + 256], KC, 256, G_NORM, 704)
            S.barrier()
            pm32 = cst[0:64, C_PM:C_PM + 64]

            xs_ring = Ring(S, nc, ph, "p1xs", [128, KC, 512], F32, 2, dma=True)
            sq_ring = Ring(S, nc, ph, "p1sq", [128, KC, 512], BF16, 1)
            hT_ring = Ring(S, nc, ph, "p1hT", [128, KC, 512], BF16, 2)
            rstd_ring = Ring(S, nc, ph, "p1rstd", [128, 512], F32, 1)
            tmp_ring = Ring(S, nc, ph, "p1tmp", [128, 512], F32, 2)
            sqs_ring = Ring(S, nc, ph, "p1sqs", [128, 512], BF16, 3)
            rr_ring = Ring(S, nc, ph, "p1rr", [128, 512], F32, 2)
            o16_ring = Ring(S, nc, ph, "p1o16", [128, 512], BF16, 6)
            vo_ring = Ring(S, nc, ph, "p1vo", [128, 4, 256], BF16, 2)
            kg_ring = Ring(S, nc, ph, "p1kg", [64, 512], F32, 4)
            cs_ring = Ring(S, nc, ph, "p1cs", [64, 2, 512], F32, 3)
            trg = self.trig_alloc(ph, 512, "k")
            pr = self.psring([0, 1, 2, 3, 4, 5, 6, 7])
            xsrc = self.xT_all.rearrange("(k p) t -> p k t", p=128)
            ngroups = self.dbg_groups or SEQ // 512

            def p1_load(g):
                xs, bxs = xs_ring.next()
                S.dma(lambda e: e.dma_start(out=xs[:], in_=xsrc[:, :, g * 512:(g + 1) * 512]), writes=[bxs])
                return xs, bxs

            prt = self.psring([5, 6, 7])
            HB = [(self.psb[i], self.pbuf[i]) for i in range(5)]

            def pre(g):
                cols = slice(g * 512, (g + 1) * 512)
                xs, bxs = p1_load(g)
                cs, bcs = cs_ring.next()
                self.trig(trg, self.pos_all[0:1, cols], cs, bcs)
                sq, bsq = sq_ring.next()
                ps, pb = prt.next()
                rstd, brstd = rstd_ring.next()
                hT, bhT = hT_ring.next()
                self.normalize_tokens(xs, bxs, sq, bsq, ps, pb, rstd[:, :], brstd, hT, bhT, tmp_ring)
                return hT, bhT, cs, bcs

            def fm(hT, bhT, c0, m, ps, pb):
                for k in range(KC):
                    self.MM(ps[0:m, :], Wk[:, k, c0:c0 + m], hT[:, k, :], k == 0, k == KC - 1, [b_Wk, bhT], [pb])

            def norm_store(pss, gain_cols, inv_n, dsts):
                sqs = []
                for ps, pb in pss:
                    s16, bs16 = sqs_ring.next()
                    self.OP("act", "activation", [pb], [bs16], out=s16[:, :], in_=ps[:, :], func=AF.Square)
                    sqs.append((s16, bs16))
                pz, pzb = prt.next()
                for i, (s16, bs16) in enumerate(sqs):
                    self.MM(pz[:, :], ones[:, :], s16[:, :], i == 0, i == len(sqs) - 1, [bs16, b_const], [pzb])
                rr, brr = rr_ring.next()
                self.rsqrt_ps(pz[:, :], pzb, 128, 512, inv_n, rr[:, :], brr, tmp_ring)
                for i, (ps, pb) in enumerate(pss):
                    o16, bo = o16_ring.next()
                    self.OP("dve", "scalar_tensor_tensor", [pb, brr, b_cst], [bo], out=o16[:, :], in0=ps[:, :],
                            scalar=self.gcol(gain_cols[i]), in1=rr[:, :], op0=ALU.mult, op1=ALU.mult)
                    dst_ap, dst_buf = dsts[i]
                    self.DMA(dst_ap, o16[:, :], [bo], [dst_buf])

            def proj(g, hT, bhT):
                cols = slice(g * 512, (g + 1) * 512)
                fm(hT, bhT, 0, 128, *HB[0])
                fm(hT, bhT, 128, 128, *HB[1])
                fm(hT, bhT, 384, 128, *HB[2])
                fm(hT, bhT, 512, 128, *HB[3])
                fm(hT, bhT, 640, 64, *HB[4])
                ps, pb = prt.next()
                fm(hT, bhT, 256, 128, ps, pb)
                o16, bo = o16_ring.next()
                self.OP("act", "activation", [pb], [bo], out=o16[:, :], in_=ps[:, :], func=AF.Copy)
                self.DMA(self.s_kiT[:, cols], o16[:, :], [bo], [self.B_kiT])
                vo, bvo = vo_ring.next()
                for tb in range(4):
                    ps, pb = prt.next()
                    for k in range(KC):
                        self.MM(ps[:, 0:256], hT[:, k, tb * 128:(tb + 1) * 128], Wk[:, k, 704:960], k == 0, k == KC - 1, [b_Wk, bhT], [pb])
                    self.OP("act", "activation", [pb], [bvo], out=vo[:, tb, :], in_=ps[:, 0:256], func=AF.Copy)
                self.DMA(self.s_VA[g * 512:(g + 1) * 512, :].rearrange("(b p) c -> p b c", p=128), vo[:], [bvo], [self.B_VA])

            def post(g, cs, bcs):
                cols = slice(g * 512, (g + 1) * 512)
                for kvh in range(2):
                    norm_store([HB[kvh]], [G_AK], 1.0 / 128, [(self.s_kTA[kvh, :, cols], self.B_kTA)])
                norm_store([HB[2], HB[3]], [G_KVLAT, G_KVLAT + 1], 1.0 / 256,
                           [(self.s_cT[0, :, cols], self.B_cT), (self.s_cT[1, :, cols], self.B_cT)])
                ps, pb = HB[4]
                s16, bs16 = sqs_ring.next()
                self.OP("act", "activation", [pb], [bs16], out=s16[0:64, :], in_=ps[0:64, :], func=AF.Square)
                self.DMA(self.s_sqpe[:, cols], s16[0:64, :], [bs16], [self.B_sqpe])
                kg, bkg = kg_ring.next()
                self.OP("dve", "tensor_scalar", [pb, b_cst], [bkg], out=kg[:, :], in0=ps[0:64, :], scalar1=self.gcol(G_BKR, 64),
                        scalar2=None, op0=ALU.mult)
                px, pxb = prt.next()
                self.MM(px[0:64, :], pm32, kg[:, :], True, True, [bkg, b_cst], [pxb])
                t1, bt1 = kg_ring.next()
                self.OP("pool", "tensor_tensor", [bkg, bcs], [bt1], out=t1[:, :], in0=kg[:, :], in1=cs[:, 0, :], op=ALU.mult)
                t2, bt2 = kg_ring.next()
                self.OP("dve", "tensor_tensor", [pxb, bcs], [bt2], out=t2[:, :], in0=px[0:64, :], in1=cs[:, 1, :], op=ALU.mult)
                o16, bo = o16_ring.next()
                self.OP("dve", "tensor_tensor", [bt1, bt2], [bo], out=o16[0:64, :], in0=t1[:, :], in1=t2[:, :], op=ALU.add)
                self.DMA(self.s_kpeR[:, cols], o16[0:64, :], [bo], [self.B_kpeR])

            cur = pre(0)
            for g in range(ngroups):
                nxt = pre(g + 1) if g + 1 < ngroups else None
                proj(g, cur[0], cur[1])
                post(g, cur[2], cur[3])
                cur = nxt
    def OP(self, eng, m, reads, writes, **kw):
        return self.S.op(eng, (m, kw), reads, writes)

    def DMA(self, out, in_, reads, writes):
        is_store = not any(b.dsem is not None for b in writes)
        return self.S.dma(("dma_start", dict(out=out, in_=in_)), reads, writes, q=("act" if is_store else "sp"))

    def MM(self, out, lhsT, rhs, start, stop, reads, writes, **extra):
        return self.S.op("pe", ("matmul", dict(out=out, lhsT=lhsT, rhs=rhs, start=start, stop=stop, **extra)), reads, writes)

    def load_hT_own(self, stack):
        hT = self.sb(stack, "hTown", [128, KC, OWN], BF16)
        b = self.S.buf(f"hTown{_uid()}", dma=True)
        for h2 in range(2):
            self.DMA(hT[:, h2 * 8:(h2 + 1) * 8, :], self.s_hT[:, h2 * 8:(h2 + 1) * 8, :], [self.B_hT], [b])
        return hT, b

    def own_proj(self, wl, wring, hT, b_hT, pr, specs, post):
        def load(i):
            ap, ncols, gain, _ = specs[i]
            wt, wb = wring.next()
            wl.load(wt, wb, ap, KC, ncols, gain)
            return wt, wb
        nxt = load(0)
        for i, spec in enumerate(specs):
            wt, wb = nxt
            if i + 1 < len(specs):
                nxt = load(i + 1)
            for (c0, m, tag, idx) in spec[3]:
                for tg in range(2):
                    ps, pb = pr.next()
                    for k in range(KC):
                        self.MM(ps[0:m, :], wt[:, k, c0:c0 + m], hT[:, k, tg * 512:(tg + 1) * 512], k == 0, k == KC - 1,
                                [wb, b_hT], [pb])
                    post(tag, idx, tg, ps, pb)

    def phaseA(self):
        S, nc = self.S, self.nc
        cst, b_cst, ones, b_const, ident, ident4 = self.cst, self.b_cst, self.ones, self.b_const, self.ident, self.ident4
        w_in = self.w_in
        nslots = self.dbg_slots or NSLOT
        with ExitStack() as pa:
            iw_abs = self.sb(pa, "iw_abs", [128, NSLOT, 16])
            iw_sgn = self.sb(pa, "iw_sgn", [128, NSLOT, 16])
            biasSel = self.sb(pa, "biasSel", [128, 9, 8, 128], BF16)
            gsc = self.sb(pa, "gscA", [128, 1])
            b_qTA, b_qiT, b_iw, b_bias, b_gsc = (S.buf(n) for n in ("qTA", "qiT", "iw", "biasSel", "gscA"))
            self.OP("dve", "tensor_scalar", [b_cst], [b_gsc], out=gsc[:, :], in0=self.gcol(G_AQ), scalar1=A_SCALE, scalar2=None,
                    op0=ALU.mult)
            with ExitStack() as sp1:
                hT, b_hT = self.load_hT_own(sp1)
                qTA = self.sb(sp1, "qTA", [128, 8, OWN], BF16)
                qiT = self.sb(sp1, "qiT", [128, 8, OWN], BF16)
                wl = self.wloader(sp1, "pa")
                wring = Ring(S, nc, sp1, "pawr", [128, KC, 256], BF16, 2)
                sqs_ring = Ring(S, nc, sp1, "pasqs", [128, 512], BF16, 2)
                rr_ring = Ring(S, nc, sp1, "parr", [128, 512], F32, 2)
                tmp_ring = Ring(S, nc, sp1, "patmp", [128, 512], F32, 2)
                so_ring = Ring(S, nc, sp1, "paso", [128, 512], BF16, 2)
                pr = self.psring([0, 1, 2, 3, 4, 5])
                prz = self.psring([6, 7])

                def post(tag, idx, tg, ps, pb):
                    cols = slice(tg * 512, (tg + 1) * 512)
                    if tag == "aq":
                        s16, bs = sqs_ring.next()
                        self.OP("act", "activation", [pb], [bs], out=s16[:, :], in_=ps[:, :], func=AF.Square)
                        pz, pzb = prz.next()
                        self.MM(pz[:, :], ones[:, :], s16[:, :], True, True, [bs, b_const], [pzb])
                        rr, brr = rr_ring.next()
                        self.rsqrt_ps(pz[:, :], pzb, 128, 512, 1.0 / 128, rr[:, :], brr, tmp_ring)
                        self.OP("dve", "scalar_tensor_tensor", [pb, brr, b_gsc], [b_qTA], out=qTA[:, idx, cols], in0=ps[:, :],
                                scalar=gsc[:, 0:1], in1=rr[:, :], op0=ALU.mult, op1=ALU.mult)
                    elif tag == "iq":
                        self.OP("act", "activation", [pb], [b_qiT], out=qiT[:, idx, cols], in_=ps[:, :], func=AF.Copy)
                    elif tag == "az":
                        so, bso = so_ring.next()
                        self.OP("act", "activation", [pb], [bso], out=so[:, :], in_=ps[:, :], func=AF.Silu)
                        self.DMA(self.s_silu[0, idx, :, cols], so[:, :], [bso], [self.B_silu[0]])

                specs = []
                for tag, o in (("aq", O_AQ), ("iq", O_IQ), ("az", O_AZ)):
                    for i in range(4):
                        specs.append((w_in[:, o + 256 * i:o + 256 * (i + 1)], 256, G_NORM,
                                      [(0, 128, tag, 2 * i), (128, 128, tag, 2 * i + 1)]))
                self.own_proj(wl, wring, hT, b_hT, pr, specs, post)
                wt, wb = wring.next()
                wl.load(wt, wb, w_in[:, O_IW:O_IW + 16], KC, 16, G_NORM)
                for tb in range(NSLOT):
                    ps, pb = pr.next()
                    for k in range(KC):
                        self.MM(ps[:, 0:16], hT[:, k, tb * 128:(tb + 1) * 128], wt[:, k, 0:16], k == 0, k == KC - 1, [wb, b_hT], [pb])
                    self.OP("act", "activation", [pb], [b_iw], out=iw_abs[:, tb, :], in_=ps[:, 0:16], func=AF.Abs)
                    self.OP("act", "activation", [pb], [b_iw], out=iw_sgn[:, tb, :], in_=ps[:, 0:16], func=AF.Sign)
                tabbc = self.sb(sp1, "tabbc", [128, 256])
                b_tab = S.buf("tabbc", dma=True)
                self.DMA(tabbc[:, :], self.relb[0:1, :].partition_broadcast(128), [], [b_tab])
                Bacc = self.sb(sp1, "Bacc", [128, 8, 256])
                eq = self.sb(sp1, "eqb", [128, 256])
                tmpB = self.sb(sp1, "tmpB", [128, 8, 128])
                b_Bacc, b_eq, b_tmpB = S.buf("Bacc"), S.buf("eqb"), S.buf("tmpB")
                bkt = cst[:, C_BK:C_BK + 256]
                tabd = self.sb(sp1, "tabd", [128, 32, 8])
                self.OP("dve", "tensor_tensor", [b_tab], [b_tab], out=tabd[:, :, :], in0=tabbc[:, :].rearrange("p (b h) -> p b h", h=8),
                        in1=tabbc[:, 248:256].unsqueeze(1).to_broadcast([128, 32, 8]), op=ALU.subtract)
                self.OP("dve", "memset", [], [b_Bacc], ap=Bacc[:, :, :], constant=0.0)
                for b in range(31):
                    self.OP("dve", "tensor_scalar", [b_cst], [b_eq], out=eq[:, :], in0=bkt, scalar1=float(b), scalar2=None,
                            op0=ALU.is_equal)
                    for h in range(8):
                        self.OP("dve", "scalar_tensor_tensor", [b_eq, b_tab, b_Bacc], [b_Bacc], out=Bacc[:, h, :], in0=eq[:, :],
                                scalar=tabd[:, b, h:h + 1], in1=Bacc[:, h, :], op0=ALU.mult, op1=ALU.add)
                for r in range(9):
                    self.OP("dve", "tensor_scalar", [b_Bacc, b_cst], [b_tmpB], out=tmpB[:, :, :], in0=Bacc[:, :, 128:256],
                            scalar1=cst[:, C_SEL + 9 + r:C_SEL + 10 + r], scalar2=None, op0=ALU.mult)
                    self.OP("dve", "scalar_tensor_tensor", [b_Bacc, b_cst, b_tmpB], [b_bias], out=biasSel[:, r, :, :],
                            in0=Bacc[:, :, 0:128], scalar=cst[:, C_SEL + r:C_SEL + r + 1], in1=tmpB[:, :, :],
                            op0=ALU.mult, op1=ALU.add)

                for h2 in range(2):
                    self.DMA(self.s_qTA[:, h2 * 4:(h2 + 1) * 4, :], qTA[:, h2 * 4:(h2 + 1) * 4, :], [b_qTA], [self.B_qTA])
                    self.DMA(self.s_qiT[:, h2 * 4:(h2 + 1) * 4, :], qiT[:, h2 * 4:(h2 + 1) * 4, :], [b_qiT], [self.B_qiT])
            S.barrier()
            if self.debug:
                self.dbg_store("iw_abs", iw_abs, [128, NSLOT, 16], F32, [b_iw])
                self.dbg_store("iw_sgn", iw_sgn, [128, NSLOT, 16], F32, [b_iw])
                self.dbg_store("biasSel", biasSel, [128, 9, 8, 128], BF16, [b_bias])

            S.scope = "PA2load"
            with ExitStack() as sp2:
                kTA = self.sb(sp2, "kTA", [128, 2, SEQ], BF16)
                vA = self.sb(sp2, "vA", [128, NB, 256], BF16)
                kiT = self.sb(sp2, "kiT", [128, SEQ], BF16)
                b_kTA, b_vA, b_kiT = S.buf("kTAs", dma=True), S.buf("vAs", dma=True), S.buf("kiTs", dma=True)
                self.DMA(kiT[:, :], self.s_kiT[:, :], [self.B_kiT], [b_kiT])
                for g in range(2):
                    self.DMA(kTA[:, g, :], self.s_kTA[g, :, :], [self.B_kTA], [b_kTA])
                vsrc = self.s_VA.rearrange("(b p) c -> p b c", p=128)
                for q4 in range(4):
                    self.DMA(vA[:, q4 * 16:(q4 + 1) * 16, :], vsrc[:, q4 * 16:(q4 + 1) * 16, :], [self.B_VA], [b_vA])
                score = self.sb(sp2, "score", [128, SEQ])
                b_score = [S.buf(f"score{i}") for i in range(SEQ // 512)]
                mask_ring = Ring(S, nc, sp2, "maskadd", [128, SEQ], BF16, 2)
                qa_ring = Ring(S, nc, sp2, "paqa", [128, 8, 128], BF16, 2, dma=True)
                qi_ring = Ring(S, nc, sp2, "paqi", [128, 8, 128], BF16, 2, dma=True)
                junk = self.sb(sp2, "junk", [128, 4096], BF16)
                b_junk = S.buf("junk")
                R_ring = Ring(S, nc, sp2, "paR", [128, 512], F32, 2)
                PT_ring = Ring(S, nc, sp2, "paPT", [128, 512], BF16, 3)
                rz_ring = Ring(S, nc, sp2, "parz", [128, 512], F32, 2)
                uo_ring = Ring(S, nc, sp2, "pauo", [128, 4, 128], BF16, 2)
                sz_ring = Ring(S, nc, sp2, "pasz", [128, 8, 128], BF16, 1, dma=True)
                sm = self.sb(sp2, "pasm", [128, 64])
                b_lo = [S.buf("lo0"), S.buf("lo1")]
                b_hi, b_w0, b_mid, b_cnt, b_gew, b_cnts, b_W = (S.buf(n) for n in ("hi", "w0", "mid", "cnt", "gew", "cnts", "W"))
                Wc = 16
                prw = self.psring([0, 1, 2, 3])

                slot = {}

                def idx(j):
                    qcols = slice(j * 128, (j + 1) * 128)
                    L = 1024 * (j + 1)
                    nch = L // 512
                    qi, b_qi = qi_ring.next()
                    qa, b_qa = qa_ring.next()
                    self.DMA(qi[:, :, :], self.s_qiT[:, :, qcols], [self.B_qiT], [b_qi])
                    self.DMA(qa[:, :, :], self.s_qTA[:, :, qcols], [self.B_qTA], [b_qa])
                    maskadd, b_mask = mask_ring.next()
                    slot[j] = dict(qa=qa, b_qa=b_qa, maskadd=maskadd, b_mask=b_mask)
                    S.scope = f"PAidx{j}"
                    for c5 in range(nch):
                        kcols = slice(c5 * 512, (c5 + 1) * 512)
                        for h in range(16):
                            hp, half = h // 2, h % 2
                            prt = slice(half * 64, half * 64 + 64)
                            ps, pb = prw.next()
                            self.MM(ps[:, :], qi[prt, hp, :], kiT[prt, kcols], True, True, [b_qi, b_kiT], [pb])
                            R, bR = R_ring.next()
                            self.OP("act", "activation", [pb, b_iw], [bR], out=R[:, :], in_=ps[:, :], func=AF.Relu,
                                    scale=iw_abs[:, j, h:h + 1])
                            if h == 0:
                                self.OP("dve", "tensor_scalar", [bR, b_iw], [b_score[c5]], out=score[:, kcols], in0=R[:, :],
                                        scalar1=iw_sgn[:, j, 0:1], scalar2=None, op0=ALU.mult)
                            else:
                                self.OP("dve", "scalar_tensor_tensor", [bR, b_iw, b_score[c5]], [b_score[c5]], out=score[:, kcols],
                                        in0=R[:, :], scalar=iw_sgn[:, j, h:h + 1], in1=score[:, kcols], op0=ALU.mult, op1=ALU.add)

                def bis(j):
                    L = 1024 * (j + 1)
                    nch = L // 512
                    maskadd, b_mask = slot[j]["maskadd"], slot[j]["b_mask"]
                    sbufs = b_score[0:nch]
                    S.scope = f"PAbis{j}"
                    self.OP("dve", "tensor_reduce", sbufs, [b_lo[0]], out=sm[:, 0:1], in_=score[:, 0:L], axis=AX.X, op=ALU.min)
                    for c5 in (nch - 2, nch - 1):
                        off = (c5 - (nch - 2)) * 512
                        self.OP("dve", "scalar_tensor_tensor", [b_cst, b_score[c5]], [b_score[c5]],
                                out=score[:, c5 * 512:(c5 + 1) * 512], in0=cst[:, C_CM + off:C_CM + off + 512], scalar=-1e30,
                                in1=score[:, c5 * 512:(c5 + 1) * 512], op0=ALU.mult, op1=ALU.add)
                    self.OP("dve", "tensor_reduce", sbufs, [b_hi], out=sm[:, 2:3], in_=score[:, 0:L], axis=AX.X, op=ALU.max)
                    self.OP("dve", "tensor_tensor", [b_hi, b_lo[0]], [b_w0], out=sm[:, 3:4], in0=sm[:, 2:3], in1=sm[:, 0:1],
                            op=ALU.subtract)
                    self.OP("dve", "tensor_scalar", [b_cst, b_w0], [b_W], out=sm[:, Wc:Wc + NITER], in0=cst[:, C_PW:C_PW + NITER],
                            scalar1=sm[:, 3:4], scalar2=None, op0=ALU.mult)
                    self.OP("dve", "tensor_tensor", [b_lo[0], b_W], [b_mid], out=sm[:, 4:5], in0=sm[:, 0:1], in1=sm[:, Wc:Wc + 1],
                            op=ALU.add)
                    npc = (L + 4095) // 4096
                    cur = 0
                    for it in range(NITER):
                        for p in range(npc):
                            c0, c1 = p * 4096, min(L, (p + 1) * 4096)
                            self.OP("dve", "tensor_scalar", b_score[c0 // 512:c1 // 512] + [b_mid], [b_junk, b_cnts],
                                    out=junk[:, 0:c1 - c0], in0=score[:, c0:c1], scalar1=sm[:, 4:5], scalar2=None,
                                    op0=ALU.is_ge, op1=ALU.add, accum_out=sm[:, 8 + p:9 + p])
                        if npc > 1:
                            self.OP("dve", "tensor_reduce", [b_cnts], [b_cnt], out=sm[:, 5:6], in_=sm[:, 8:8 + npc], axis=AX.X,
                                    op=ALU.add)
                            csrc, bcs = sm[:, 5:6], b_cnt
                        else:
                            csrc, bcs = sm[:, 8:9], b_cnts
                        self.OP("dve", "tensor_scalar", [bcs, b_W], [b_gew], out=sm[:, 6:7], in0=csrc, scalar1=TOPK - 0.5,
                                scalar2=sm[:, Wc + it:Wc + it + 1], op0=ALU.is_ge, op1=ALU.mult)
                        nxt = 1 - cur
                        if it + 1 < NITER:
                            self.OP("dve", "scalar_tensor_tensor", [b_gew, b_W, b_lo[cur]], [b_mid], out=sm[:, 4:5], in0=sm[:, 6:7],
                                    scalar=sm[:, Wc + it + 1:Wc + it + 2], in1=sm[:, cur:cur + 1], op0=ALU.add, op1=ALU.add)
                        self.OP("dve", "tensor_tensor", [b_gew, b_lo[cur]], [b_lo[nxt]], out=sm[:, nxt:nxt + 1], in0=sm[:, 6:7],
                                in1=sm[:, cur:cur + 1], op=ALU.add)
                        cur = nxt
                    self.OP("dve", "tensor_scalar", sbufs + [b_lo[cur]], [b_mask], out=maskadd[:, 0:L], in0=score[:, 0:L],
                            scalar1=sm[:, cur:cur + 1], scalar2=NEG, op0=ALU.is_lt, op1=ALU.mult)
                    if self.debug:
                        self.dbg_store(f"thr{j}", sm[:, cur:cur + 1], [128, 1], F32, [b_lo[cur]], raw_ap=True)

                def att(j):
                    qcols = slice(j * 128, (j + 1) * 128)
                    qa, b_qa = slot[j]["qa"], slot[j]["b_qa"]
                    maskadd, b_mask = slot[j]["maskadd"], slot[j]["b_mask"]
                    S.scope = f"PAatt{j}"
                    sz, bsz = sz_ring.next()
                    self.DMA(sz[:, :, :], self.s_silu[0, :, :, qcols].rearrange("h p t -> p h t"), [self.B_silu[0]], [bsz])
                    nkb = 8 * (j + 1)
                    psO = [(self.psb[4], self.pbuf[4]), (self.psb[5], self.pbuf[5])]
                    psZ = [(self.psb[6], self.pbuf[6]), (self.psb[7], self.pbuf[7])]
                    units = [(kb, g) for kb in range(nkb) for g in range(2)]
                    st = {}

                    def qk(u):
                        kb, g = units[u]
                        ks = slice(kb * 128, (kb + 1) * 128)
                        r = kb - (8 * j - 1)
                        near = 0 <= r <= 8
                        ps, pb = prw.next()
                        self.MM(ps[:, :], kTA[:, g, ks], qa[:, 4 * g:4 * g + 4, :], True, False, [b_kTA, b_qa], [pb])
                        self.MM(ps[:, :], maskadd[:, ks], ident4[:, :], False, not near, [b_mask, b_const], [pb])
                        if near:
                            self.MM(ps[:, :], ident[:, :], biasSel[:, r, 4 * g:4 * g + 4, :], False, True, [b_bias, b_const], [pb])
                        st[u] = (ps, pb)

                    def ex_pv(u):
                        kb, g = units[u]
                        ps, pb = st.pop(u)
                        PT, bPT = PT_ring.next()
                        self.OP("act", "activation", [pb], [bPT], out=PT[:, :], in_=ps[:, :], func=AF.Exp)
                        self.MM(psO[g][0][:, :], vA[:, kb, g * 128:(g + 1) * 128], PT[:, :], kb == 0, kb == nkb - 1,
                                [b_vA, bPT], [psO[g][1]])
                        self.MM(psZ[g][0][:, :], ones[:, :], PT[:, :], kb == 0, kb == nkb - 1, [b_const, bPT], [psZ[g][1]])
                    LOOK = 3
                    for u in range(min(LOOK, len(units))):
                        qk(u)
                    for u in range(len(units)):
                        if u + LOOK < len(units):
                            qk(u + LOOK)
                        ex_pv(u)
                    for g in range(2):
                        rz, brz = rz_ring.next()
                        self.OP("dve", "reciprocal", [psZ[g][1]], [brz], out=rz[:, :], in_=psZ[g][0][:, :])
                        rz2, brz2 = rz_ring.next()
                        self.OP("dve", "tensor_tensor", [psO[g][1], brz], [brz2], out=rz2[:, :], in0=psO[g][0][:, :], in1=rz[:, :],
                                op=ALU.mult)
                        uo, buo = uo_ring.next()
                        self.OP("pool", "tensor_tensor", [brz2, bsz], [buo], out=uo[:, :, :],
                                in0=rz2[:, :].rearrange("p (h t) -> p h t", h=4), in1=sz[:, 4 * g:4 * g + 4, :], op=ALU.mult)
                        self.DMA(self.s_u[0, 4 * g:4 * g + 4, :, qcols].rearrange("h p t -> p h t"), uo[:, :, :], [buo], [self.B_u[0]])


                idx(0)
                bis(0)
                for j in range(nslots):
                    if j + 1 < nslots:
                        idx(j + 1)
                        bis(j + 1)
                    att(j)

    def dbg_store(self, name, tile, shape, dt, reads, raw_ap=False):
        d = self.nc.dram_tensor("dbg_" + name, shape, dt, kind="ExternalOutput").ap()
        src = tile if raw_ap else tile[:]
        idx = tuple(slice(None) for _ in shape)
        self.DMA(d[idx], src, reads, [self.S.buf("dbg_" + name)])

    def phaseB(self):
        S, nc = self.S, self.nc
        cst, b_cst, ones, b_const, ident = self.cst, self.b_cst, self.ones, self.b_const, self.ident
        w_in = self.w_in
        nheads = self.dbg_slots or 8
        with ExitStack() as pbs:
            qNT = self.sb(pbs, "qNT", [128, 8, OWN], BF16)
            qRT = self.sb(pbs, "qRT", [64, 8, OWN], BF16)
            b_qNT, b_qRT = S.buf("qNT"), S.buf("qRT")
            gs = self.sb(pbs, "gscB", [128, 2])
            b_gs = S.buf("gscB")
            self.OP("dve", "scalar_tensor_tensor", [b_cst], [b_gs], out=gs[:, 0:1], in0=self.gcol(G_BQN), scalar=B_SCALE,
                    in1=self.gcol(G_BKN), op0=ALU.mult, op1=ALU.mult)
            self.OP("dve", "tensor_scalar", [b_cst], [b_gs], out=gs[0:64, 1:2], in0=self.gcol(G_BQR, 64), scalar1=B_SCALE, scalar2=None,
                    op0=ALU.mult)
            with ExitStack() as sp1:
                hT, b_hT = self.load_hT_own(sp1)
                wl = self.wloader(sp1, "pb")
                wring = Ring(S, nc, sp1, "pbwr", [128, KC, 256], BF16, 2)
                cs_own = self.sb(sp1, "cs_own", [64, 2, OWN])
                b_cs = S.buf("cs_own")
                trg = self.trig_alloc(sp1, OWN, "o")
                self.trig(trg, self.pos_own[0:1, :], cs_own, b_cs)
                cqT = self.sb(sp1, "cqT", [128, 4, OWN], BF16)
                b_cqT = S.buf("cqT")
                rq_lat = self.sb(sp1, "rq_lat", [128, OWN])
                b_rql = S.buf("rq_lat")
                wuq = self.sb(sp1, "wuq", [128, 4, 1536], BF16)
                b_wuq = S.buf("wuq")
                sqs_ring = Ring(S, nc, sp1, "pbsqs", [128, 512], BF16, 3)
                tmp_ring = Ring(S, nc, sp1, "pbtmp", [128, 512], F32, 2)
                so_ring = Ring(S, nc, sp1, "pbso", [128, 512], BF16, 2)
                qn_ring = Ring(S, nc, sp1, "pbqn", [128, 512], F32, 2)
                qr_ring = Ring(S, nc, sp1, "pbqr", [64, 512], F32, 4)
                rr_ring = Ring(S, nc, sp1, "pbrr", [128, 512], F32, 2)
                pr = self.psring([0, 1, 2, 3, 4, 5])
                pz = [(self.psb[6], self.pbuf[6]), (self.psb[7], self.pbuf[7])]
                for i in range(6):
                    wl.load(wuq, b_wuq, self.w_uq[:, 256 * i:256 * (i + 1)], 4, 256, G_QLAT, 256 * i)

                def post(tag, idx, tg, ps, pb):
                    cols = slice(tg * 512, (tg + 1) * 512)
                    if tag == "bcq":
                        self.OP("act", "activation", [pb], [b_cqT], out=cqT[:, idx, cols], in_=ps[:, :], func=AF.Copy)
                        s16, bs = sqs_ring.next()
                        self.OP("act", "activation", [pb], [bs], out=s16[:, :], in_=ps[:, :], func=AF.Square)
                        self.MM(pz[tg][0][:, :], ones[:, :], s16[:, :], idx == 0, idx == 3, [bs, b_const], [pz[tg][1]])
                        if idx == 3:
                            self.rsqrt_ps(pz[tg][0][:, :], pz[tg][1], 128, 512, 1.0 / 512, rq_lat[:, cols], b_rql, tmp_ring)
                    elif tag == "bz":
                        so, bso = so_ring.next()
                        self.OP("act", "activation", [pb], [bso], out=so[:, :], in_=ps[:, :], func=AF.Silu)
                        self.DMA(self.s_silu[1, idx, :, cols], so[:, :], [bso], [self.B_silu[1]])

                specs = []
                for i in range(2):
                    specs.append((w_in[:, O_BCQ + 256 * i:O_BCQ + 256 * (i + 1)], 256, G_NORM,
                                  [(0, 128, "bcq", 2 * i), (128, 128, "bcq", 2 * i + 1)]))
                for i in range(4):
                    specs.append((w_in[:, O_BZ + 256 * i:O_BZ + 256 * (i + 1)], 256, G_NORM,
                                  [(0, 128, "bz", 2 * i), (128, 128, "bz", 2 * i + 1)]))
                self.own_proj(wl, wring, hT, b_hT, pr, specs, post)
                pm32 = cst[0:64, C_PM:C_PM + 64]
                for h in range(8):
                    for tg in range(2):
                        cols = slice(tg * 512, (tg + 1) * 512)
                        psN, pbN = pr.next()
                        psR, pbR = pr.next()
                        for k in range(4):
                            self.MM(psN[:, :], wuq[:, k, h * 192:h * 192 + 128], cqT[:, k, cols], k == 0, k == 3, [b_wuq, b_cqT], [pbN])
                        for k in range(4):
                            self.MM(psR[0:64, :], wuq[:, k, h * 192 + 128:h * 192 + 192], cqT[:, k, cols], k == 0, k == 3,
                                    [b_wuq, b_cqT], [pbR])
                        qn, bqn = qn_ring.next()
                        qr, bqr = qr_ring.next()
                        self.OP("dve", "tensor_tensor", [pbN, b_rql], [bqn], out=qn[:, :], in0=psN[:, :], in1=rq_lat[:, cols], op=ALU.mult)
                        self.OP("dve", "tensor_tensor", [pbR, b_rql], [bqr], out=qr[:, :], in0=psR[0:64, :], in1=rq_lat[0:64, cols],
                                op=ALU.mult)
                        s1, bs1 = sqs_ring.next()
                        s2, bs2 = sqs_ring.next()
                        self.OP("act", "activation", [bqn], [bs1], out=s1[:, :], in_=qn[:, :], func=AF.Square)
                        self.OP("act", "activation", [bqr], [bs2], out=s2[0:64, :], in_=qr[:, :], func=AF.Square)
                        pq, pqb = pr.next()
                        self.MM(pq[:, :], ones[:, :], s1[:, :], True, False, [bs1, b_const], [pqb])
                        self.MM(pq[:, :], ones[0:64, :], s2[0:64, :], False, True, [bs2, b_const], [pqb])
                        rr, brr = rr_ring.next()
                        self.rsqrt_ps(pq[:, :], pqb, 128, 512, 1.0 / 192, rr[:, :], brr, tmp_ring)
                        self.OP("dve", "scalar_tensor_tensor", [bqn, brr, b_gs], [b_qNT], out=qNT[:, h, cols], in0=qn[:, :],
                                scalar=gs[:, 0:1], in1=rr[:, :], op0=ALU.mult, op1=ALU.mult)
                        qg, bqg = qr_ring.next()
                        self.OP("dve", "scalar_tensor_tensor", [bqr, brr, b_gs], [bqg], out=qg[:, :], in0=qr[:, :],
                                scalar=gs[0:64, 1:2], in1=rr[0:64, :], op0=ALU.mult, op1=ALU.mult)
                        px, pxb = pr.next()
                        self.MM(px[0:64, :], pm32, qg[:, :], True, True, [bqg, b_cst], [pxb])
                        t1, bt1 = qr_ring.next()
                        self.OP("pool", "tensor_tensor", [bqg, b_cs], [bt1], out=t1[:, :], in0=qg[:, :], in1=cs_own[:, 0, cols], op=ALU.mult)
                        t2, bt2 = qr_ring.next()
                        self.OP("dve", "tensor_tensor", [pxb, b_cs], [bt2], out=t2[:, :], in0=px[0:64, :], in1=cs_own[:, 1, cols], op=ALU.mult)
                        self.OP("dve", "tensor_tensor", [bt1, bt2], [b_qRT], out=qRT[:, h, cols], in0=t1[:, :], in1=t2[:, :], op=ALU.add)
            S.barrier()
            if self.debug:
                self.dbg_store("qNT", qNT, [128, 8, OWN], BF16, [b_qNT])
                self.dbg_store("qRT", qRT, [64, 8, OWN], BF16, [b_qRT])

            S.scope = "PB2load"
            with ExitStack() as sp2:
                cT = self.sb(sp2, "cT", [128, 2, SEQ], BF16)
                kpeR = self.sb(sp2, "kpeR", [64, SEQ], BF16)
                sqpe = self.sb(sp2, "sqpe", [64, SEQ], BF16)
                b_cT, b_kpeR, b_sqpe = S.buf("cTs", dma=True), S.buf("kpeRs", dma=True), S.buf("sqpes", dma=True)
                for g in range(2):
                    self.DMA(cT[:, g, :], self.s_cT[g, :, :], [self.B_cT], [b_cT])
                self.DMA(kpeR[:, :], self.s_kpeR[:, :], [self.B_kpeR], [b_kpeR])
                self.DMA(sqpe[:, :], self.s_sqpe[:, :], [self.B_sqpe], [b_sqpe])
                wukv = self.sb(sp2, "wukv", [128, 2, 2048], BF16)
                b_wukv = S.buf("wukv")
                with ExitStack() as wls:
                    wl = self.wloader(wls, "pb2")
                    for i in range(8):
                        wl.load(wukv, b_wukv, self.w_ukv[:, 256 * i:256 * (i + 1)], 2, 256, None, 256 * i)
                S.barrier()
                cmB = self.sb(sp2, "cmB", [128, 8, 128], BF16)
                b_cmB = S.buf("cmB")
                self.OP("dve", "tensor_scalar", [b_cst], [b_cmB], out=cmB[:, :, :],
                        in0=cst[:, C_CM:C_CM + 1024].rearrange("p (r s) -> p r s", r=8), scalar1=NEG, scalar2=None, op0=ALU.mult)
                KnT = self.sb(sp2, "KnT", [128, SEQ], BF16)
                KrT = self.sb(sp2, "KrT", [64, SEQ], BF16)
                Vh = self.sb(sp2, "Vh", [128, NB, 128], BF16)
                b_KnT, b_KrT, b_Vh = S.buf("KnT"), S.buf("KrT"), S.buf("Vh")
                sqs_ring = Ring(S, nc, sp2, "pb2sqs", [128, 512], BF16, 3)
                tmp_ring = Ring(S, nc, sp2, "pb2tmp", [128, 512], F32, 2)
                rr_ring = Ring(S, nc, sp2, "pb2rr", [128, 512], F32, 2)
                PT_ring = Ring(S, nc, sp2, "pb2PT", [128, 512], BF16, 4)
                rz_ring = Ring(S, nc, sp2, "pb2rz", [128, 512], F32, 2)
                uo_ring = Ring(S, nc, sp2, "pb2uo", [128, 512], BF16, 2)
                sz_ring = Ring(S, nc, sp2, "pb2sz", [128, 512], BF16, 2, dma=True)
                prw = self.psring([0, 1, 2, 3])
                prb = self.psring([0, 1, 2, 3, 4, 5, 6, 7])
                psO = [(self.psb[4], self.pbuf[4]), (self.psb[5], self.pbuf[5])]
                psZ = [(self.psb[6], self.pbuf[6]), (self.psb[7], self.pbuf[7])]
                ngrp = self.dbg_groups or SEQ // 512
                nkb = ngrp * 4
                for h in range(nheads):
                    S.scope = f"PBbuild{h}"
                    for g in range(ngrp):
                        cols = slice(g * 512, (g + 1) * 512)
                        psK, pbK = prb.next()
                        for i in range(2):
                            self.MM(psK[:, :], wukv[:, i, h * 256:h * 256 + 128], cT[:, i, cols], i == 0, i == 1, [b_wukv, b_cT], [pbK])
                        s16, bs = sqs_ring.next()
                        self.OP("act", "activation", [pbK], [bs], out=s16[:, :], in_=psK[:, :], func=AF.Square)
                        pq, pqb = prb.next()
                        self.MM(pq[:, :], ones[:, :], s16[:, :], True, False, [bs, b_const], [pqb])
                        self.MM(pq[:, :], ones[0:64, :], sqpe[:, cols], False, True, [b_sqpe, b_const], [pqb])
                        rr, brr = rr_ring.next()
                        self.rsqrt_ps(pq[:, :], pqb, 128, 512, 1.0 / 192, rr[:, :], brr, tmp_ring)
                        self.OP("dve", "tensor_tensor", [pbK, brr], [b_KnT], out=KnT[:, cols], in0=psK[:, :], in1=rr[:, :], op=ALU.mult)
                        self.OP("pool", "tensor_tensor", [b_kpeR, brr], [b_KrT], out=KrT[:, cols], in0=kpeR[:, cols], in1=rr[0:64, :],
                                op=ALU.mult)
                        psV, pbV = prb.next()
                        for tb in range(4):
                            for i in range(2):
                                self.MM(psV[:, tb * 128:(tb + 1) * 128], cT[:, i, g * 512 + tb * 128:g * 512 + (tb + 1) * 128],
                                        wukv[:, i, h * 256 + 128:h * 256 + 256], i == 0, i == 1, [b_wukv, b_cT], [pbV])
                        self.OP("act", "activation", [pbV], [b_Vh], out=Vh[:, g * 4:(g + 1) * 4, :],
                                in_=psV[:, :].rearrange("p (b d) -> p b d", b=4), func=AF.Copy)
                    S.scope = f"PBatt{h}"
                    units = []
                    for kb in range(nkb):
                        c_lo = (kb // 8) * 128
                        pieces = [(c_lo, 512), (512, OWN)] if c_lo < 512 else [(c_lo, OWN)]
                        for (c0, c1) in pieces:
                            units.append((kb, c0, c1, c0 == c_lo))
                    st = {}

                    def qk(u):
                        kb, c0, c1, diag = units[u]
                        ks = slice(kb * 128, (kb + 1) * 128)
                        n = c1 - c0
                        ps, pb = prw.next()
                        self.MM(ps[:, 0:n], KnT[:, ks], qNT[:, h, c0:c1], True, False, [b_KnT, b_qNT], [pb])
                        self.MM(ps[:, 0:n], KrT[:, ks], qRT[:, h, c0:c1], False, not diag, [b_KrT, b_qRT], [pb])
                        if diag:
                            self.MM(ps[:, 0:128], cmB[:, kb % 8, :], ident[:, :], False, True, [b_cmB, b_const], [pb],
                                    skip_group_check=True)
                        st[u] = (ps, pb)

                    def ex_pv(u):
                        kb, c0, c1, diag = units[u]
                        n = c1 - c0
                        bank = c0 // 512
                        o0 = c0 - bank * 512
                        ps, pb = st.pop(u)
                        PT, bPT = PT_ring.next()
                        self.OP("act", "activation", [pb], [bPT], out=PT[:, 0:n], in_=ps[:, 0:n], func=AF.Exp)
                        self.MM(psO[bank][0][:, o0:o0 + n], Vh[:, kb, :], PT[:, 0:n], kb == 0, kb == nkb - 1,
                                [b_Vh, bPT], [psO[bank][1]], skip_group_check=True)
                        self.MM(psZ[bank][0][:, o0:o0 + n], ones[:, :], PT[:, 0:n], kb == 0, kb == nkb - 1,
                                [b_const, bPT], [psZ[bank][1]], skip_group_check=True)
                    LOOK = 3
                    for u in range(min(LOOK, len(units))):
                        qk(u)
                    for u in range(len(units)):
                        if u + LOOK < len(units):
                            qk(u + LOOK)
                        ex_pv(u)
                    for tg in range(2):
                        cols = slice(tg * 512, (tg + 1) * 512)
                        sz, bsz = sz_ring.next()
                        self.DMA(sz[:, :], self.s_silu[1, h, :, cols], [self.B_silu[1]], [bsz])
                        rz, brz = rz_ring.next()
                        self.OP("dve", "reciprocal", [psZ[tg][1]], [brz], out=rz[:, :], in_=psZ[tg][0][:, :])
                        rz2, brz2 = rz_ring.next()
                        self.OP("dve", "tensor_tensor", [psO[tg][1], brz], [brz2], out=rz2[:, :], in0=psO[tg][0][:, :], in1=rz[:, :],
                                op=ALU.mult)
                        uo, buo = uo_ring.next()
                        self.OP("pool", "tensor_tensor", [brz2, bsz], [buo], out=uo[:, :], in0=rz2[:, :], in1=sz[:, :], op=ALU.mult)
                        self.DMA(self.s_u[1, h, :, cols], uo[:, :], [buo], [self.B_u[1]])

    def phaseC(self):
        S, nc = self.S, self.nc
        cst, b_cst, ones, b_const = self.cst, self.b_cst, self.ones, self.b_const
        w_in = self.w_in
        with ExitStack() as pc:
            ckT = self.sb(pc, "ckT", [128, 4, 2, 256], BF16)
            cv = self.sb(pc, "cv", [128, 2, 1024], BF16)
            cqT = self.sb(pc, "cqTc", [128, 8, OWN], BF16)
            b_ckT, b_cv, b_cqT = S.buf("ckT"), S.buf("cv"), S.buf("cqTc")
            gs = self.sb(pc, "gscC", [128, 2])
            b_gs = S.buf("gscC")
            self.OP("dve", "tensor_scalar", [b_cst], [b_gs], out=gs[:, 0:2], in0=cst[:, G_CQ:G_CQ + 2], scalar1=C_SCALE, scalar2=None,
                    op0=ALU.mult)
            sqs_ring = Ring(S, nc, pc, "pcsqs", [128, 512], BF16, 3)
            tmp_ring = Ring(S, nc, pc, "pctmp", [128, 512], F32, 2)
            rr_ring = Ring(S, nc, pc, "pcrr", [128, 512], F32, 2)
            with ExitStack() as sp0:
                wl = self.wloader(sp0, "pc0")
                wring = Ring(S, nc, sp0, "pc0wr", [128, KC, 256], BF16, 2)
                ms = self.sb(sp0, "pcms", [128, KC, 256])
                b_ms = S.buf("pcms", dma=True)
                self.DMA(ms[:, :, :], self.memT.rearrange("(k p) m -> p k m", p=128), [], [b_ms])
                msq = self.sb(sp0, "pcmsq", [128, KC, 256], BF16)
                hm = self.sb(sp0, "pchm", [128, KC, 256], BF16)
                rm = self.sb(sp0, "pcrm", [128, 256])
                b_msq, b_hm, b_rm = S.buf("pcmsq"), S.buf("pchm"), S.buf("pcrm")
                self.OP("act", "activation", [b_ms], [b_msq], out=msq[:, :, :], in_=ms[:, :, :], func=AF.Square)
                ps, pb = self.psb[0], self.pbuf[0]
                for k in range(KC):
                    self.MM(ps[:, 0:256], ones[:, :], msq[:, k, :], k == 0, k == KC - 1, [b_msq, b_const], [pb])
                self.rsqrt_ps(ps[:, 0:256], pb, 128, 256, 1.0 / D, rm[:, :], b_rm, tmp_ring)
                self.OP("dve", "tensor_tensor", [b_ms, b_rm], [b_hm], out=hm[:, :, :], in0=ms[:, :, :],
                        in1=rm[:, :].unsqueeze(1).to_broadcast([128, KC, 256]), op=ALU.mult)
                pr = self.psring([1, 2, 3, 4, 5, 6])

                def load(i):
                    wt, wb = wring.next()
                    wl.load(wt, wb, self.w_mem[:, 256 * i:256 * (i + 1)], KC, 256, G_MEM)
                    return wt, wb
                nxt = load(0)
                for i in range(8):
                    wt, wb = nxt
                    if i + 1 < 8:
                        nxt = load(i + 1)
                    if i < 4:
                        pss = []
                        for dc in range(2):
                            p2, pb2 = pr.next()
                            for k in range(KC):
                                self.MM(p2[:, 0:256], wt[:, k, dc * 128:(dc + 1) * 128], hm[:, k, :], k == 0, k == KC - 1, [wb, b_hm], [pb2])
                            pss.append((p2, pb2))
                        pz, pzb = pr.next()
                        for dc, (p2, pb2) in enumerate(pss):
                            s16, bs = sqs_ring.next()
                            self.OP("act", "activation", [pb2], [bs], out=s16[:, 0:256], in_=p2[:, 0:256], func=AF.Square)
                            self.MM(pz[:, 0:256], ones[:, :], s16[:, 0:256], dc == 0, dc == 1, [bs, b_const], [pzb])
                        rr, brr = rr_ring.next()
                        self.rsqrt_ps(pz[:, 0:256], pzb, 128, 256, 1.0 / 256, rr[:, 0:256], brr, tmp_ring)
                        for dc, (p2, pb2) in enumerate(pss):
                            self.OP("dve", "scalar_tensor_tensor", [pb2, brr, b_cst], [b_ckT], out=ckT[:, i, dc, :], in0=p2[:, 0:256],
                                    scalar=self.gcol(G_CK + dc), in1=rr[:, 0:256], op0=ALU.mult, op1=ALU.mult)
                    else:
                        for mb in range(2):
                            p2, pb2 = pr.next()
                            for k in range(KC):
                                self.MM(p2[:, 0:256], hm[:, k, mb * 128:(mb + 1) * 128], wt[:, k, :], k == 0, k == KC - 1, [wb, b_hm], [pb2])
                            self.OP("act", "activation", [pb2], [b_cv], out=cv[:, mb, (i - 4) * 256:(i - 3) * 256], in_=p2[:, 0:256],
                                    func=AF.Copy)
            S.barrier()
            with ExitStack() as sp1:
                hT, b_hT = self.load_hT_own(sp1)
                wl = self.wloader(sp1, "pc1")
                wring = Ring(S, nc, sp1, "pc1wr", [128, KC, 256], BF16, 2)
                so_ring = Ring(S, nc, sp1, "pcso", [128, 512], BF16, 2)
                pr = self.psring([0, 1, 2, 3, 4, 5])
                prz = self.psring([6, 7])
                held = {}

                def post(tag, idx, tg, ps, pb):
                    cols = slice(tg * 512, (tg + 1) * 512)
                    if tag == "cz":
                        so, bso = so_ring.next()
                        self.OP("act", "activation", [pb], [bso], out=so[:, :], in_=ps[:, :], func=AF.Silu)
                        self.DMA(self.s_silu[2, idx, :, cols], so[:, :], [bso], [self.B_silu[2]])
                        return
                    held[(idx % 2, tg)] = (ps, pb)
                    if idx % 2 == 1 and tg == 1:
                        for t2 in range(2):
                            c2 = slice(t2 * 512, (t2 + 1) * 512)
                            pz, pzb = prz.next()
                            for dc in range(2):
                                p2, pb2 = held[(dc, t2)]
                                s16, bs = sqs_ring.next()
                                self.OP("act", "activation", [pb2], [bs], out=s16[:, :], in_=p2[:, :], func=AF.Square)
                                self.MM(pz[:, :], ones[:, :], s16[:, :], dc == 0, dc == 1, [bs, b_const], [pzb])
                            rr, brr = rr_ring.next()
                            self.rsqrt_ps(pz[:, :], pzb, 128, 512, 1.0 / 256, rr[:, :], brr, tmp_ring)
                            for dc in range(2):
                                p2, pb2 = held[(dc, t2)]
                                self.OP("dve", "scalar_tensor_tensor", [pb2, brr, b_gs], [b_cqT], out=cqT[:, idx - 1 + dc, c2],
                                        in0=p2[:, :], scalar=gs[:, dc:dc + 1], in1=rr[:, :], op0=ALU.mult, op1=ALU.mult)

                specs = []
                for i in range(4):
                    specs.append((w_in[:, O_CQ + 256 * i:O_CQ + 256 * (i + 1)], 256, G_NORM,
                                  [(0, 128, "cq", 2 * i), (128, 128, "cq", 2 * i + 1)]))
                for i in range(4):
                    specs.append((w_in[:, O_CZ + 256 * i:O_CZ + 256 * (i + 1)], 256, G_NORM,
                                  [(0, 128, "cz", 2 * i), (128, 128, "cz", 2 * i + 1)]))
                self.own_proj(wl, wring, hT, b_hT, pr, specs, post)
            S.barrier()
            with ExitStack() as sp2:
                PT_ring = Ring(S, nc, sp2, "pcPT", [128, 512], BF16, 3)
                rz_ring = Ring(S, nc, sp2, "pcrz", [128, 512], F32, 2)
                uo_ring = Ring(S, nc, sp2, "pcuo", [128, 512], BF16, 2)
                sz_ring = Ring(S, nc, sp2, "pcsz", [128, 512], BF16, 2, dma=True)
                prw = self.psring([0, 1, 2])
                for h in range(4):
                    for tg in range(2):
                        cols = slice(tg * 512, (tg + 1) * 512)
                        psO = [(self.psb[4], self.pbuf[4]), (self.psb[5], self.pbuf[5])]
                        psZ = (self.psb[6], self.pbuf[6])
                        for mb in range(2):
                            ps, pb = prw.next()
                            for dc in range(2):
                                self.MM(ps[:, :], ckT[:, h, dc, mb * 128:(mb + 1) * 128], cqT[:, 2 * h + dc, cols], dc == 0, dc == 1,
                                        [b_ckT, b_cqT], [pb])
                            PT, bPT = PT_ring.next()
                            self.OP("act", "activation", [pb], [bPT], out=PT[:, :], in_=ps[:, :], func=AF.Exp)
                            for dvc in range(2):
                                self.MM(psO[dvc][0][:, :], cv[:, mb, h * 256 + dvc * 128:h * 256 + (dvc + 1) * 128], PT[:, :], mb == 0, mb == 1,
                                        [b_cv, bPT], [psO[dvc][1]])
                            self.MM(psZ[0][:, :], ones[:, :], PT[:, :], mb == 0, mb == 1, [b_const, bPT], [psZ[1]])
                        rz, brz = rz_ring.next()
                        self.OP("dve", "reciprocal", [psZ[1]], [brz], out=rz[:, :], in_=psZ[0][:, :])
                        for dvc in range(2):
                            sz, bsz = sz_ring.next()
                            self.DMA(sz[:, :], self.s_silu[2, 2 * h + dvc, :, cols], [self.B_silu[2]], [bsz])
                            rz2, brz2 = rz_ring.next()
                            self.OP("dve", "tensor_tensor", [psO[dvc][1], brz], [brz2], out=rz2[:, :], in0=psO[dvc][0][:, :], in1=rz[:, :],
                                    op=ALU.mult)
                            uo, buo = uo_ring.next()
                            self.OP("pool", "tensor_tensor", [brz2, bsz], [buo], out=uo[:, :], in0=rz2[:, :], in1=sz[:, :], op=ALU.mult)
                            self.DMA(self.s_u[2, 2 * h + dvc, :, cols], uo[:, :], [buo], [self.B_u[2]])

    def phaseF(self):
        S, nc = self.S, self.nc
        cst, b_cst = self.cst, self.b_cst
        w_in = self.w_in
        with ExitStack() as pf:
            mT = self.sb(pf, "mergedT", [128, KC, OWN], BF16)
            b_mT = [S.buf(f"mergedT{i}") for i in range(KC)]
            wl = self.wloader(pf, "pf")
            wring = Ring(S, nc, pf, "pfwr", [128, KC, 256], BF16, 4)
            with ExitStack() as sp1:
                hT, b_hT = self.load_hT_own(sp1)
                uT = self.sb(sp1, "uT", [128, 24, OWN], BF16)
                b_uT = S.buf("uT", dma=True)
                for n in range(3):
                    self.DMA(uT[:, n * 8:(n + 1) * 8, :], self.s_u[n].rearrange("c p t -> p c t"), [self.B_u[n]], [b_uT])
                sg_ring = Ring(S, nc, sp1, "pfsg", [128, 512], F32, 2)
                mac_ring = Ring(S, nc, sp1, "pfmac", [128, 512], F32, 4)
                macs = {}
                tm_ring = Ring(S, nc, sp1, "pftm", [128, 512], F32, 2)
                pr = self.psring([0, 1, 2, 3, 4, 5, 6, 7])
                jobs = []
                for ds in range(8):
                    for n in range(3):
                        jobs.append(("br", ds, n))
                        jobs.append(("gt", ds, n))

                def load(job):
                    kind, ds, n = job
                    wt, wb = wring.next()
                    if kind == "br":
                        wl.load(wt, wb, self.w_br[n, :, ds * 256:(ds + 1) * 256], 8, 256, None)
                    else:
                        wl.load(wt, wb, w_in[:, O_G + n * D + ds * 256:O_G + n * D + (ds + 1) * 256], KC, 256, G_NORM)
                    return wt, wb
                loaded = [load(jobs[0]), load(jobs[1])]
                for ji in range(0, len(jobs), 2):
                    _, ds, n = jobs[ji]
                    (wbr, bwbr), (wgt, bwgt) = loaded
                    loaded = [load(jobs[ji + 2]), load(jobs[ji + 3])] if ji + 2 < len(jobs) else None
                    for dcl in range(2):
                        dc = ds * 2 + dcl
                        for tg in range(2):
                            cols = slice(tg * 512, (tg + 1) * 512)
                            pg, pgb = pr.next()
                            for k in range(KC):
                                self.MM(pg[:, :], wgt[:, k, dcl * 128:(dcl + 1) * 128], hT[:, k, cols], k == 0, k == KC - 1, [bwgt, b_hT], [pgb])
                            sg, bsg = sg_ring.next()
                            self.OP("act", "activation", [pgb], [bsg], out=sg[:, :], in_=pg[:, :], func=AF.Sigmoid)
                            py, pyb = pr.next()
                            for k in range(8):
                                self.MM(py[:, :], wbr[:, k, dcl * 128:(dcl + 1) * 128], uT[:, n * 8 + k, cols], k == 0, k == 7, [bwbr, b_uT], [pyb])
                            key = (dc, tg)
                            if n == 0:
                                mac, bmac = mac_ring.next()
                                macs[key] = (mac, bmac)
                                self.OP("dve", "tensor_tensor", [pyb, bsg], [bmac], out=mac[:, :], in0=py[:, :], in1=sg[:, :], op=ALU.mult)
                            else:
                                mac, bmac = macs[key]
                                tm, btm = tm_ring.next()
                                self.OP("dve", "tensor_tensor", [pyb, bsg], [btm], out=tm[:, :], in0=py[:, :], in1=sg[:, :], op=ALU.mult)
                                if n == 1:
                                    self.OP("pool", "tensor_tensor", [btm, bmac], [bmac], out=mac[:, :], in0=mac[:, :], in1=tm[:, :], op=ALU.add)
                                else:
                                    self.OP("pool", "tensor_tensor", [btm, bmac], [b_mT[dc]], out=mT[:, dc, cols], in0=mac[:, :], in1=tm[:, :],
                                            op=ALU.add)
            S.barrier()
            S.scope = "PF2"
            with ExitStack() as sp2:
                xo_ring = Ring(S, nc, sp2, "pfxo", [128, OWN], F32, 2, dma=True)
                oo_ring = Ring(S, nc, sp2, "pfoo", [128, 512], F32, 3)
                pr = self.psring([0, 1, 2, 3, 4, 5, 6, 7])
                xsrc = self.xT_own.rearrange("(k p) t -> p k t", p=128)
                osrc = self.outT.rearrange("(k p) t -> p k t", p=128)

                def load(i):
                    wt, wb = wring.next()
                    wl.load(wt, wb, self.w_out[:, i * 256:(i + 1) * 256], KC, 256, None)
                    return wt, wb
                nxt = load(0)
                for i in range(8):
                    wt, wb = nxt
                    if i + 1 < 8:
                        nxt = load(i + 1)
                    for ocl in range(2):
                        oc = 2 * i + ocl
                        xo, bxo = xo_ring.next()
                        self.DMA(xo[:, :], xsrc[:, oc, :], [], [bxo])
                        for tg in range(2):
                            cols = slice(tg * 512, (tg + 1) * 512)
                            ps, pb = pr.next()
                            for k in range(KC):
                                self.MM(ps[:, :], wt[:, k, ocl * 128:(ocl + 1) * 128], mT[:, k, cols], k == 0, k == KC - 1, [wb, b_mT[k]], [pb])
                            oo, boo = oo_ring.next()
                            self.OP("dve", "tensor_tensor", [pb, bxo], [boo], out=oo[:, :], in0=ps[:, :], in1=xo[:, cols], op=ALU.add)
                            self.DMA(osrc[:, oc, cols], oo[:, :], [boo], [self.B_out])

def _t5_bucket_np(n):
    n = np.maximum(n, 0)
    nf = np.maximum(n, 1).astype(np.float32)
    large = 16 + (np.log(nf / np.float32(16)) / np.float32(math.log(128 / 16)) * np.float32(16)).astype(np.int32)
    large = np.minimum(large, 31)
    return np.where(n < 16, n, large)


def _const_tile(core, gains):
    c = np.zeros((128, NCST), np.float32)
    c[:, C_ID:C_ID + 128] = np.eye(128, dtype=np.float32)
    t = np.arange(128)[:, None, None]
    r = np.arange(8)[None, :, None]
    s = np.arange(128)[None, None, :]
    masked = (r > core) | ((r == core) & (s > t))
    c[:, C_CM:C_CM + 1024] = masked.astype(np.float32).reshape(128, 1024)
    ss = np.arange(128)[:, None, None]
    dl = np.arange(2)[None, :, None]
    tt = np.arange(128)[None, None, :]
    c[:, C_BK:C_BK + 256] = _t5_bucket_np(128 * dl + tt - ss).astype(np.float32).reshape(128, 256)
    c[:, C_G:C_G + gains.shape[1]] = gains
    invf = 1.0 / (np.float32(10000.0) ** (np.arange(0, 64, 2, dtype=np.float32) / np.float32(64)))
    c[0:64, C_INVF] = np.concatenate([invf, invf]).astype(np.float32)
    pm = np.zeros((64, 64), np.float32)
    for m in range(32):
        pm[m + 32, m] = -1.0
        pm[m, m + 32] = 1.0
    c[0:64, C_PM:C_PM + 64] = pm
    sel = np.zeros(18, np.float32)
    for rr in range(9):
        dlt = core + 1 - rr
        if dlt == 0:
            sel[rr] = 1.0
        if dlt == 1:
            sel[9 + rr] = 1.0
    c[:, C_SEL:C_SEL + 18] = sel[None, :]
    c[:, C_PW:C_PW + NITER] = (0.5 ** np.arange(1, NITER + 1, dtype=np.float32))[None, :]
    return c


def _pack_gains(norm_g, mem_norm_g, a_q, a_k, b_qlat, b_kvlat, b_q, b_k, c_q, c_k):
    g = np.zeros((128, C_INVF - C_G), np.float32)

    def put(col, vec, n):
        v = np.asarray(vec, np.float32).reshape(-1)
        nch = (v.size + 127) // 128
        for i in range(nch):
            seg = v[i * 128:(i + 1) * 128]
            g[:seg.size, col - C_G + i] = seg
    put(G_NORM, norm_g, 16)
    put(G_MEM, mem_norm_g, 16)
    put(G_AQ, a_q, 1)
    put(G_AK, a_k, 1)
    put(G_QLAT, b_qlat, 4)
    put(G_KVLAT, b_kvlat, 2)
    bq = np.asarray(b_q, np.float32).reshape(-1)
    bk = np.asarray(b_k, np.float32).reshape(-1)
    put(G_BQN, bq[:128], 1)
    put(G_BQR, bq[128:], 1)
    put(G_BKN, bk[:128], 1)
    put(G_BKR, bk[128:], 1)
    put(G_CQ, c_q, 2)
    put(G_CK, c_k, 2)
    return g


_PROG_CACHE = {}


def make_in_maps(x, mem, positions, rel_bias, norm_g, mem_norm_g, w_in, a_q_norm_g, a_k_norm_g, b_q_lat_norm_g,
                 b_kv_lat_norm_g, w_b_uq, w_b_ukv, b_q_norm_g, b_k_norm_g, w_mem_kv, c_q_norm_g, c_k_norm_g,
                 w_branch, w_out):
    x2 = np.asarray(x, np.float32)[0]
    xT = np.ascontiguousarray(x2.T)
    pos = np.asarray(positions, np.int32)[0]
    gains = _pack_gains(norm_g[0], mem_norm_g[0], a_q_norm_g[0], a_k_norm_g[0], b_q_lat_norm_g[0], b_kv_lat_norm_g[0],
                        b_q_norm_g[0], b_k_norm_g[0], c_q_norm_g[0], c_k_norm_g[0])
    shared = {
        "xT_all": xT,
        "pos_all": np.ascontiguousarray(pos[None, :]),
        "memT": np.ascontiguousarray(np.asarray(mem, np.float32)[0].T),
        "w_in": np.ascontiguousarray(np.asarray(w_in, np.float32)[0]),
        "w_uq": np.ascontiguousarray(np.asarray(w_b_uq, np.float32)[0]),
        "w_ukv": np.ascontiguousarray(np.asarray(w_b_ukv, np.float32)[0]),
        "w_mem": np.ascontiguousarray(np.asarray(w_mem_kv, np.float32)[0]),
        "w_br": np.ascontiguousarray(np.asarray(w_branch, np.float32)[0]),
        "w_out": np.ascontiguousarray(np.asarray(w_out, np.float32)[0]),
        "relb": np.ascontiguousarray(np.asarray(rel_bias, np.float32).reshape(1, 256)),
    }
    in_maps = []
    own_idx = []
    for c in range(NCORES):
        idx = np.concatenate([np.arange((c + 8 * j) * 128, (c + 8 * j + 1) * 128) for j in range(NSLOT)])
        own_idx.append(idx)
        m = dict(shared)
        m["xT_own"] = np.ascontiguousarray(xT[:, idx])
        m["pos_own"] = np.ascontiguousarray(pos[idx][None, :])
        m["cst"] = _const_tile(c, gains)
        in_maps.append(m)
    return in_maps, own_idx


def kernel(**inputs):
    in_maps, own_idx = make_in_maps(**inputs)
    if "prog" not in _PROG_CACHE:
        _PROG_CACHE["prog"] = Prog().build()
    nc = _PROG_CACHE["prog"]
    res = run_bass_kernel_spmd(nc, in_maps, core_ids=list(range(NCORES)))
    out = np.zeros((1, SEQ, D), np.float32)
    for c in range(NCORES):
        out[0, own_idx[c], :] = np.asarray(res.results[c]["outT"], np.float32).T
    return out
```
